# Optimizing a Trainium2 kernel written in Bass

```python
import math
import jax, jax.numpy as jnp
from jax import lax
import numpy as np

D_MODEL = 1024
BATCH = 2
SEQ = 8192
DEPTH = 4

N_HEADS = 16
N_KV_HEADS = 4
HEAD_DIM = 64
GROUP = N_HEADS // N_KV_HEADS
WINDOW = 128
BLOCK = 128
ROPE_THETA = 10000.0
D_RNN = 1024
RNN_BLOCKS = 4
RNN_BLOCK_W = D_RNN // RNN_BLOCKS
CONV_W = 4
LRU_C = 8.0
D_FF = 2816
DEEPNORM_ALPHA = (2.0 * DEPTH) ** 0.25
DEEPNORM_BETA = (8.0 * DEPTH) ** -0.25
LN_EPS = 1e-5
N_MIXERS = 2
N_ATTN_LAYERS = (DEPTH + 1) // 2
N_LRU_LAYERS = DEPTH // 2
QKV_COLS = (N_HEADS + 2 * N_KV_HEADS) * HEAD_DIM

kernel_name = "hybrid_swa_sink_rglru_macaron_deepnorm"


def _layernorm(x, g, b):
    xf = x.astype(jnp.float32)
    mu = jnp.mean(xf, axis=-1, keepdims=True)
    xc = xf - mu
    var = jnp.mean(xc * xc, axis=-1, keepdims=True)
    y = xc * lax.rsqrt(var + LN_EPS) * g.astype(jnp.float32) + b.astype(jnp.float32)
    return y.astype(x.dtype)


def _swiglu(x, w_in, w_out):
    g, u = jnp.split(x @ w_in, 2, axis=-1)
    return (jax.nn.silu(g) * u) @ w_out


def _rope(t, cos, sin):
    t1, t2 = jnp.split(t, 2, axis=-1)
    c = cos[None, :, None, :]
    s = sin[None, :, None, :]
    out = jnp.concatenate([t1 * c - t2 * s, t2 * c + t1 * s], axis=-1)
    return out.astype(t.dtype)


def _band(t):
    prev = jnp.pad(t[:, :-1], ((0, 0), (1, 0), (0, 0), (0, 0), (0, 0)))
    return jnp.concatenate([prev, t], axis=2)


def _swa_sink_attention(x, w_qkv, sinks, w_o, cos, sin):
    B, S, _ = x.shape
    nb = S // BLOCK
    qkv = x @ w_qkv
    q, k, v = jnp.split(qkv, [N_HEADS * HEAD_DIM, (N_HEADS + N_KV_HEADS) * HEAD_DIM], axis=-1)
    q = _rope(q.reshape(B, S, N_HEADS, HEAD_DIM), cos, sin)
    k = _rope(k.reshape(B, S, N_KV_HEADS, HEAD_DIM), cos, sin)
    v = v.reshape(B, S, N_KV_HEADS, HEAD_DIM)
    q = q.reshape(B, nb, BLOCK, N_KV_HEADS, GROUP, HEAD_DIM)
    kb = _band(k.reshape(B, nb, BLOCK, N_KV_HEADS, HEAD_DIM))
    vb = _band(v.reshape(B, nb, BLOCK, N_KV_HEADS, HEAD_DIM))
    s = jnp.einsum('bnqkgd,bnjkd->bnkgqj', q, kb).astype(jnp.float32) * (HEAD_DIM ** -0.5)
    qi = jnp.arange(BLOCK)[:, None]
    kj = jnp.arange(2 * BLOCK)[None, :]
    dist = qi + BLOCK - kj
    in_window = (dist >= 0) & (dist < WINDOW)
    k_pos = jnp.arange(nb)[:, None] * BLOCK - BLOCK + jnp.arange(2 * BLOCK)[None, :]
    mask = in_window[None, :, :] & (k_pos >= 0)[:, None, :]
    s = jnp.where(mask[None, :, None, None, :, :], s, jnp.finfo(jnp.float32).min)
    sink = sinks.astype(jnp.float32).reshape(N_KV_HEADS, GROUP)[None, None, :, :, None, None]
    m = jnp.maximum(jnp.max(s, axis=-1, keepdims=True), sink)
    p = jnp.exp(s - m)
    denom = jnp.sum(p, axis=-1, keepdims=True) + jnp.exp(sink - m)
    p = (p / denom).astype(x.dtype)
    o = jnp.einsum('bnkgqj,bnjkd->bnqkgd', p, vb).reshape(B, S, N_HEADS * HEAD_DIM)
    return o @ w_o


def _lru_combine(left, right):
    a1, b1 = left
    a2, b2 = right
    return a1 * a2, a2 * b1 + b2


def _rglru_block(x, w_in, conv_w, conv_b, w_ra, b_ra, w_rx, b_rx, lam, w_out):
    B, S, _ = x.shape
    xb, gb = jnp.split(x @ w_in, 2, axis=-1)
    gate = jax.nn.gelu(gb)
    xc = lax.conv_general_dilated(
        xb, conv_w[:, None, :].astype(xb.dtype), window_strides=(1,), padding=[(CONV_W - 1, 0)],
        dimension_numbers=('NWC', 'WIO', 'NWC'), feature_group_count=D_RNN) + conv_b
    xr = xc.reshape(B, S, RNN_BLOCKS, RNN_BLOCK_W)
    r = jax.nn.sigmoid(jnp.einsum('bsnc,ncd->bsnd', xr, w_ra).reshape(B, S, D_RNN) + b_ra)
    i = jax.nn.sigmoid(jnp.einsum('bsnc,ncd->bsnd', xr, w_rx).reshape(B, S, D_RNN) + b_rx)
    log_a = LRU_C * r.astype(jnp.float32) * jax.nn.log_sigmoid(lam.astype(jnp.float32))
    a = jnp.exp(log_a)
    b = jnp.sqrt(-jnp.expm1(2.0 * log_a)) * (i * xc).astype(jnp.float32)
    _, h = lax.associative_scan(_lru_combine, (a, b), axis=1)
    y = h.astype(x.dtype) * gate
    return y @ w_out


def setup_inputs(seed: int = 0) -> dict:
    key = jax.random.key(seed)
    ks = jax.random.split(key, 24)
    f32 = jnp.float32
    nrm = lambda k, shape, scale: jax.random.normal(k, shape, f32) * scale
    x = jax.random.normal(ks[0], (BATCH, SEQ, D_MODEL), f32)
    ffn1_w_in = nrm(ks[1], (DEPTH, D_MODEL, 2 * D_FF), D_MODEL ** -0.5)
    ffn1_w_out = nrm(ks[2], (DEPTH, D_FF, D_MODEL), D_FF ** -0.5 * DEEPNORM_BETA)
    ffn2_w_in = nrm(ks[3], (DEPTH, D_MODEL, 2 * D_FF), D_MODEL ** -0.5)
    ffn2_w_out = nrm(ks[4], (DEPTH, D_FF, D_MODEL), D_FF ** -0.5 * DEEPNORM_BETA)
    ln_g = 1.0 + nrm(ks[5], (DEPTH, 3, D_MODEL), 0.02)
    ln_b = nrm(ks[6], (DEPTH, 3, D_MODEL), 0.02)
    attn_w_qkv = nrm(ks[7], (N_ATTN_LAYERS, D_MODEL, QKV_COLS), D_MODEL ** -0.5)
    attn_sinks = nrm(ks[8], (N_ATTN_LAYERS, N_HEADS), 0.5)
    attn_w_o = nrm(ks[9], (N_ATTN_LAYERS, N_HEADS * HEAD_DIM, D_MODEL), (N_HEADS * HEAD_DIM) ** -0.5 * DEEPNORM_BETA)
    lru_w_in = nrm(ks[10], (N_LRU_LAYERS, D_MODEL, 2 * D_RNN), D_MODEL ** -0.5)
    lru_conv_w = nrm(ks[11], (N_LRU_LAYERS, CONV_W, D_RNN), CONV_W ** -0.5)
    lru_conv_b = nrm(ks[12], (N_LRU_LAYERS, D_RNN), 0.01)
    lru_w_ra = nrm(ks[13], (N_LRU_LAYERS, RNN_BLOCKS, RNN_BLOCK_W, RNN_BLOCK_W), RNN_BLOCK_W ** -0.5)
    lru_b_ra = nrm(ks[14], (N_LRU_LAYERS, D_RNN), 0.01)
    lru_w_rx = nrm(ks[15], (N_LRU_LAYERS, RNN_BLOCKS, RNN_BLOCK_W, RNN_BLOCK_W), RNN_BLOCK_W ** -0.5)
    lru_b_rx = nrm(ks[16], (N_LRU_LAYERS, D_RNN), 0.01)
    a_c = jax.random.uniform(ks[17], (N_LRU_LAYERS, D_RNN), f32, 0.9, 0.999)
    sig = a_c ** (1.0 / LRU_C)
    lru_lambda = jnp.log(sig) - jnp.log1p(-sig)
    lru_w_out = nrm(ks[18], (N_LRU_LAYERS, D_RNN, D_MODEL), D_RNN ** -0.5 * DEEPNORM_BETA)
    return {"x": x, "ffn1_w_in": ffn1_w_in, "ffn1_w_out": ffn1_w_out,
            "ffn2_w_in": ffn2_w_in, "ffn2_w_out": ffn2_w_out, "ln_g": ln_g, "ln_b": ln_b,
            "attn_w_qkv": attn_w_qkv, "attn_sinks": attn_sinks, "attn_w_o": attn_w_o,
            "lru_w_in": lru_w_in, "lru_conv_w": lru_conv_w, "lru_conv_b": lru_conv_b,
            "lru_w_ra": lru_w_ra, "lru_b_ra": lru_b_ra, "lru_w_rx": lru_w_rx, "lru_b_rx": lru_b_rx,
            "lru_lambda": lru_lambda, "lru_w_out": lru_w_out}


def reference(x, ffn1_w_in, ffn1_w_out, ffn2_w_in, ffn2_w_out, ln_g, ln_b,
              attn_w_qkv, attn_sinks, attn_w_o,
              lru_w_in, lru_conv_w, lru_conv_b, lru_w_ra, lru_b_ra, lru_w_rx, lru_b_rx,
              lru_lambda, lru_w_out):
    S = x.shape[1]
    pos = jnp.arange(S, dtype=jnp.float32)
    inv_freq = ROPE_THETA ** (-jnp.arange(0, HEAD_DIM, 2, dtype=jnp.float32) / HEAD_DIM)
    ang = pos[:, None] * inv_freq[None, :]
    cos, sin = jnp.cos(ang), jnp.sin(ang)
    h = x
    for i in range(DEPTH):
        h = _layernorm(DEEPNORM_ALPHA * h + 0.5 * _swiglu(h, ffn1_w_in[i], ffn1_w_out[i]), ln_g[i, 0], ln_b[i, 0])
        j = i // N_MIXERS
        if i % N_MIXERS == 0:
            mix = _swa_sink_attention(h, attn_w_qkv[j], attn_sinks[j], attn_w_o[j], cos, sin)
        else:
            mix = _rglru_block(h, lru_w_in[j], lru_conv_w[j], lru_conv_b[j], lru_w_ra[j], lru_b_ra[j],
                               lru_w_rx[j], lru_b_rx[j], lru_lambda[j], lru_w_out[j])
        h = _layernorm(DEEPNORM_ALPHA * h + mix, ln_g[i, 1], ln_b[i, 1])
        h = _layernorm(DEEPNORM_ALPHA * h + 0.5 * _swiglu(h, ffn2_w_in[i], ffn2_w_out[i]), ln_g[i, 2], ln_b[i, 2])
    return h
```

```python
import numpy as np
import concourse.bass as bass
import concourse.mybir as mybir
from concourse.bass_utils import run_bass_kernel_spmd

F32 = mybir.dt.float32
BF16 = mybir.dt.bfloat16
AF = mybir.ActivationFunctionType
ALU = mybir.AluOpType
AX = mybir.AxisListType

NCORES = 8
T = 2048
NT = 16
D = 1024
KC = 8
DFF = 2816
NF = 22
DEPTH = 4
ALPHA = (2.0 * DEPTH) ** 0.25
EPS = 1e-5
NEG = -30000.0
GROUPS = [(0, 8), (8, 7), (15, 7)]
GMAX = 8


class Tok:
    __slots__ = ("sem", "cnt", "own")

    def __init__(self, sem, cnt, own):
        self.sem, self.cnt, self.own = sem, cnt, own


class Buf:
    __slots__ = ("w", "r", "dsem", "dcnt", "name", "keep")

    def __init__(self, name="", keep=False):
        self.w = None
        self.r = {}
        self.dsem = None
        self.dcnt = 0
        self.name = name
        self.keep = keep


class Eng:
    def __init__(self, nc, name, h, selfsync):
        self.nc, self.name, self.h, self.selfsync = nc, name, h, selfsync
        self.gen = 0
        self.sem = nc.alloc_semaphore(f"{name}_p{self.gen}")
        self.cnt = 0
        self.seen = {}

    def wait(self, tok):
        if tok is None:
            return
        if tok.own == self.name and not self.selfsync:
            return
        k = id(tok.sem)
        if self.seen.get(k, 0) >= tok.cnt:
            return
        self.h.wait_ge(tok.sem, tok.cnt)
        self.seen[k] = tok.cnt

    def mark(self, ins):
        if self.cnt >= 30000:
            self.gen += 1
            self.sem = self.nc.alloc_semaphore(f"{self.name}_p{self.gen}")
            self.cnt = 0
        self.cnt += 1
        ins.then_inc(self.sem, 1)
        return Tok(self.sem, self.cnt, self.name)

    def cur(self):
        return Tok(self.sem, self.cnt, self.name) if self.cnt > 0 else None


class Ctx:
    def __init__(self, nc):
        self.nc = nc
        self.pe = Eng(nc, "pe", nc.tensor, False)
        self.act = Eng(nc, "act", nc.scalar, True)
        self.dve = Eng(nc, "dve", nc.vector, True)
        self.pool = Eng(nc, "pool", nc.gpsimd, True)
        self.sp = Eng(nc, "sp", nc.sync, False)
        self.engs = [self.pe, self.act, self.dve, self.pool, self.sp]
        self.nsem = 0
        self.free_dsems = []
        self.phase_dbufs = []

    def _pre(self, eng, reads, writes):
        for b in reads:
            eng.wait(b.w)
        for b in writes:
            eng.wait(b.w)
            for t in b.r.values():
                eng.wait(t)

    def _post(self, tok, reads, writes):
        for b in reads:
            b.r[id(tok.sem)] = tok
        for b in writes:
            b.w = tok
            b.r = {}

    def op(self, eng, fn, reads=(), writes=()):
        self._pre(eng, reads, writes)
        ins = fn()
        tok = eng.mark(ins)
        self._post(tok, reads, writes)
        return ins

    def mm(self, fns, reads, writes):
        eng = self.pe
        self._pre(eng, reads, writes)
        ins = None
        for fn in fns:
            ins = fn()
        tok = eng.mark(ins)
        self._post(tok, reads, writes)

    def dma(self, q, out, in_, reads, writes, sbuf):
        self._pre(q, reads, writes)
        ins = q.h.dma_start(out=out, in_=in_)
        if sbuf.dsem is None:
            if self.free_dsems:
                sbuf.dsem, sbuf.dcnt = self.free_dsems.pop()
            else:
                sbuf.dsem = self.nc.alloc_semaphore(f"dma{self.nsem}")
                self.nsem += 1
            if not sbuf.keep:
                self.phase_dbufs.append(sbuf)
        sbuf.dcnt += 16
        ins.then_inc(sbuf.dsem, 16)
        tok = Tok(sbuf.dsem, sbuf.dcnt, "dma")
        self._post(tok, reads, writes)
        return tok

    def barrier(self):
        toks = [e.cur() for e in self.engs]
        for e in self.engs:
            for t in toks:
                if t is not None and t.own != e.name:
                    e.wait(t)
        for b in self.phase_dbufs:
            t = Tok(b.dsem, b.dcnt, "dma")
            self.pool.wait(t)
            self.sp.wait(t)
            self.free_dsems.append((b.dsem, b.dcnt))
            b.dsem = None
        self.phase_dbufs = []


def build_program(layers=(0, 1, 2, 3), n_sub=None):
    nc = bass.Bass("TRN2", target_bir_lowering=False)
    K = Ctx(nc)
    pe, act, dve, pool, sp = K.pe, K.act, K.dve, K.pool, K.sp
    V, S, P, G = nc.vector, nc.scalar, nc.tensor, nc.gpsimd

    def din(name, shape):
        return nc.dram_tensor(name, list(shape), F32, kind="ExternalInput").ap()

    x_d = din("x", [T, D])
    w1i = din("ffn1_w_in", [DEPTH, D, 2 * DFF])
    w1o = din("ffn1_w_out", [DEPTH, DFF, D])
    w2i = din("ffn2_w_in", [DEPTH, D, 2 * DFF])
    w2o = din("ffn2_w_out", [DEPTH, DFF, D])
    lng = din("ln_g", [DEPTH, 3, D])
    lnb = din("ln_b", [DEPTH, 3, D])
    wqkv = din("attn_w_qkv", [2, D, 1536])
    sinks_d = din("attn_sinks", [2, 16])
    wo_d = din("attn_w_o", [2, D, D])
    lwin = din("lru_w_in", [2, D, 2048])
    lcw = din("lru_conv_w", [2, 4, D])
    lcb = din("lru_conv_b", [2, D])
    lwra = din("lru_w_ra", [2, 4, 256, 256])
    lbra = din("lru_b_ra", [2, D])
    lwrx = din("lru_w_rx", [2, 4, 256, 256])
    lbrx = din("lru_b_rx", [2, D])
    llam = din("lru_lambda", [2, D])
    lwout = din("lru_w_out", [2, D, D])
    cos_d = din("cos_t", [T, 32])
    sin_d = din("sin_t", [T, 32])
    mask0_d = din("mask0", [128, 256])
    maskn_d = din("maskn", [128, 256])
    sel_d = din("sel", [128, 8])
    wsel_d = din("wsel", [128, 8])
    out_d = nc.dram_tensor("out", [T, D], F32, kind="ExternalOutput").ap()

    X = nc.alloc_sbuf_tensor("X", [128, NT, D], F32).ap()
    XT = nc.alloc_sbuf_tensor("XT", [128, KC, T], BF16).ap()
    ident = nc.alloc_sbuf_tensor("ident", [128, 128], F32).ap()
    identb = nc.alloc_sbuf_tensor("identb", [128, 128], BF16).ap()
    st6 = nc.alloc_sbuf_tensor("st6", [128, 2, 2, 6], F32).ap()
    mv4 = nc.alloc_sbuf_tensor("mv4", [128, 2, 4, 2], F32).ap()
    sd4 = nc.alloc_sbuf_tensor("sd4", [128, 2, 4, 3], F32).ap()
    epsc = nc.alloc_sbuf_tensor("epsc", [128, 1], F32).ap()
    onec = nc.alloc_sbuf_tensor("onec", [128, 1], F32).ap()
    selt = nc.alloc_sbuf_tensor("selt", [128, 8], F32).ap()
    wselt = nc.alloc_sbuf_tensor("wselt", [128, 8], F32).ap()
    ARENA_W = 27400
    arena = nc.alloc_sbuf_tensor("arena", [128, ARENA_W], F32).ap()
    PS = nc.alloc_psum_tensor("PS", [128, 4096], F32).ap()

    class Arena:
        def __init__(self):
            self.off = 0

        def take(self, nbytes, dtype, shape_str=None, **kw):
            assert nbytes % 4 == 0
            w = nbytes // 4
            assert self.off + w <= ARENA_W, (self.off, w)
            v = arena[:, self.off:self.off + w]
            self.off += w
            if dtype == BF16:
                v = v.bitcast(BF16)
            if shape_str:
                v = v.rearrange(shape_str, **kw)
            return v

    GBh = {"ap": None}
    bX = [Buf(f"X{t}", keep=True) for t in range(NT)]
    bXT = [Buf(f"XT{t}") for t in range(NT)]
    bGB = Buf("GB", keep=True)
    bConst = Buf("const")
    bOut = Buf("out", keep=True)
    bSt = [Buf("st0"), Buf("st1")]
    bMv = [Buf("mv0"), Buf("mv1")]
    bSd = [Buf("sd0"), Buf("sd1")]

    K.op(pool, lambda: G.memset(ident, 0.0), writes=[bConst])
    K.op(pool, lambda: G.affine_select(out=ident, in_=ident, pattern=[[-1, 128]], compare_op=ALU.not_equal,
                                       fill=1.0, base=0, channel_multiplier=1), writes=[bConst])
    K.op(pool, lambda: G.tensor_copy(out=identb, in_=ident), reads=[bConst], writes=[bConst])
    K.op(pool, lambda: G.memset(epsc, EPS), writes=[bConst])
    K.op(pool, lambda: G.memset(onec, 1.0), writes=[bConst])
    bSel = Buf("sel", keep=True)
    K.dma(sp, selt, sel_d, [], [bSel], bSel)
    bWsel = Buf("wsel", keep=True)
    K.dma(sp, wselt, wsel_d, [], [bWsel], bWsel)

    xv = x_d.rearrange("(t p) d -> p t d", p=128)
    for t in range(NT):
        K.dma(sp, X[:, t, :], xv[:, t, :], [], [bX[t]], bX[t])

    def transposes_to_XT(tt, psbuf, ps_ap):
        pv = ps_ap.rearrange("p (c q) -> p c q", c=KC)
        K.mm([(lambda c=c: P.transpose(out=pv[:, c, :], in_=X[:, tt, c * 128:(c + 1) * 128], identity=ident))
              for c in range(KC)], reads=[bX[tt], bConst], writes=[psbuf])
        K.op(act, lambda: S.activation(out=XT[:, :, tt * 128:(tt + 1) * 128], in_=pv, func=AF.Copy),
             reads=[psbuf], writes=[bXT[tt]])

    ln_state = {"idx": 0}

    def load_ln_params(i, j, GB):
        GBh["ap"] = GB
        K.dma(sp, GB[:, 0, :], lng[i, j:j + 1, :].to_broadcast([128, D]), [], [bGB], bGB)
        K.dma(sp, GB[:, 1, :], lnb[i, j:j + 1, :].to_broadcast([128, D]), [], [bGB], bGB)

    def resid_accum(tt, ybuf, y_ap, first):
        if first:
            K.op(dve, lambda: V.scalar_tensor_tensor(out=X[:, tt, :], in0=X[:, tt, :], scalar=ALPHA, in1=y_ap,
                                                     op0=ALU.mult, op1=ALU.add), reads=[ybuf], writes=[bX[tt]])
        else:
            K.op(dve, lambda: V.tensor_tensor(out=X[:, tt, :], in0=X[:, tt, :], in1=y_ap, op=ALU.add),
                 reads=[ybuf], writes=[bX[tt]])

    def ln_stats(tt):
        par = (tt // 4) % 2
        j = tt % 4
        st = st6[:, j % 2]
        bs = bSt[j % 2]
        K.op(dve, lambda: V.bn_stats(out=st[:, 0, :], in_=X[:, tt, 0:512]), reads=[bX[tt]], writes=[bs])
        K.op(dve, lambda: V.bn_stats(out=st[:, 1, :], in_=X[:, tt, 512:1024]), reads=[bX[tt], bs], writes=[bs])
        K.op(dve, lambda: V.bn_aggr(out=mv4[:, par, j, :], in_=st.rearrange("p a b -> p (a b)")), reads=[bs], writes=[bMv[par]])

    def ln_finish(tb, final_out=False, tr_ps=None):
        par = tb % 2
        sd = sd4[:, par]
        K.op(act, lambda: S.activation(out=sd[:, :, 0], in_=mv4[:, par, :, 1], func=AF.Sqrt, bias=epsc, scale=1.0),
             reads=[bMv[par], bConst], writes=[bSd[par]])
        K.op(dve, lambda: V.reciprocal(out=sd[:, :, 1], in_=sd[:, :, 0]), reads=[bSd[par]], writes=[bSd[par]])
        K.op(dve, lambda: V.scalar_tensor_tensor(out=sd[:, :, 2], in0=mv4[:, par, :, 0], scalar=-1.0, in1=sd[:, :, 1],
                                                 op0=ALU.mult, op1=ALU.mult), reads=[bSd[par], bMv[par]], writes=[bSd[par]])
        for j in range(4):
            tt = 4 * tb + j
            K.op(act, lambda: S.activation(out=X[:, tt, :], in_=X[:, tt, :], func=AF.Identity,
                                           bias=sd[:, j, 2:3], scale=sd[:, j, 1:2]),
                 reads=[bSd[par]], writes=[bX[tt]])
            K.op(dve, lambda: V.tensor_tensor(out=X[:, tt, :], in0=X[:, tt, :], in1=GBh["ap"][:, 0, :], op=ALU.mult),
                 reads=[bGB], writes=[bX[tt]])
            K.op(dve, lambda: V.tensor_tensor(out=X[:, tt, :], in0=X[:, tt, :], in1=GBh["ap"][:, 1, :], op=ALU.add),
                 reads=[bGB], writes=[bX[tt]])
            if final_out:
                K.dma(sp, out_d.rearrange("(t p) d -> p t d", p=128)[:, tt, :], X[:, tt, :], [bX[tt]], [], bOut)

    bPS0 = [Buf("ps_init0"), Buf("ps_init1")]
    for tt in range(NT):
        transposes_to_XT(tt, bPS0[tt % 2], PS[:, (tt % 2) * 1024:(tt % 2) * 1024 + 1024])

    sub_count = {"n": 0}
    total_sub = 3 * len(layers)
    sync_state = {"n": 0}
    SYT = nc.alloc_sbuf_tensor("SYT", [128, 8], F32).ap()
    bSYT = Buf("SYT", keep=True)
    K.op(pool, lambda: G.memset(SYT, 0.0), writes=[bSYT])

    def sync_cores():
        i = sync_state["n"]
        sync_state["n"] += 1
        si = nc.dram_tensor(f"sync_i{i}", [128, 8], F32)
        so = nc.dram_tensor(f"sync_o{i}", [8 * 128, 8], F32)
        bi, bo = Buf(), Buf()
        K.dma(pool, si.ap(), SYT, [bSYT], [bi], bi)
        K._pre(pool, [bi], [bo])
        cc = G.collective_compute("AllGather", ALU.bypass, replica_groups=[list(range(NCORES))],
                                  ins=[si.ap().opt()], outs=[so.ap().opt()])
        K._post(pool.mark(cc), [bi], [bo])

    def done():
        return n_sub is not None and sub_count["n"] >= n_sub

    def ffn_phase(w_in, w_out, li, lnj, is_final):
        K.barrier()
        sync_cores()
        A = Arena()
        hT = A.take(GMAX * T * 2, BF16, "p (f t) -> p f t", f=GMAX)
        WI = [A.take(KC * 256 * 2, BF16, "p (k two n) -> p k two n", k=KC, two=2) for _ in range(3)]
        WO = [A.take(GMAX * D * 2, BF16, "p (f n) -> p f n", f=GMAX) for _ in range(2)]
        SG = [A.take(512 * 4, F32) for _ in range(2)]
        GBa = A.take(2 * D * 4, F32, "p (a n) -> p a n", a=2)
        bWI = [Buf(f"WI{s}") for s in range(3)]
        bWO = [Buf(f"WO{s}") for s in range(2)]
        bSG = [Buf("SG0"), Buf("SG1")]
        bh = [[Buf(f"h{f}_{tb}") for tb in range(4)] for f in range(GMAX)]
        bPA = [Buf("PA0"), Buf("PA1")]
        bPB = [Buf("PB0"), Buf("PB1")]
        PA = [PS[:, 0:1024], PS[:, 1024:2048]]
        PB = [PS[:, 2048:3072], PS[:, 3072:4096]]
        load_ln_params(li, lnj, GBa)
        wiv = w_in.rearrange("(k p) (two n) -> p k two n", p=128, two=2)
        wov = w_out.rearrange("(f p) n -> p f n", p=128)

        def load_wi(f):
            s = f % 3
            for two in range(2):
                K.dma(pool, WI[s][:, :, two, :], wiv[:, :, two, f * 128:(f + 1) * 128], [], [bWI[s]], bWI[s])

        def load_wo(gi):
            f0, cnt = GROUPS[gi]
            s = gi % 2
            K.dma(pool, WO[s][:, 0:cnt, :], wov[:, f0:f0 + cnt, :], [], [bWO[s]], bWO[s])

        load_wi(0)
        load_wi(1)
        load_wo(0)
        it = 0
        pend_tr = []
        for gi, (f0, cnt) in enumerate(GROUPS):
            first, last = gi == 0, gi == len(GROUPS) - 1
            for fl in range(cnt):
                f = f0 + fl
                if f + 2 < NF:
                    load_wi(f + 2)
                s = f % 3
                for tb in range(4):
                    pa = PA[it % 2]
                    bpa = bPA[it % 2]
                    rhs_b = [bXT[4 * tb + q] for q in range(4)]
                    fns = []
                    for two in range(2):
                        for k in range(KC):
                            fns.append(lambda two=two, k=k: P.matmul(
                                pa[:, two * 512:(two + 1) * 512], lhsT=WI[s][:, k, two, :],
                                rhs=XT[:, k, tb * 512:(tb + 1) * 512], start=(k == 0), stop=(k == KC - 1)))
                    K.mm(fns, reads=[bWI[s]] + rhs_b, writes=[bpa])
                    sg = SG[it % 2]
                    K.op(act, lambda: S.activation(out=sg, in_=pa[:, 0:512], func=AF.Silu), reads=[bpa], writes=[bSG[it % 2]])
                    K.op(dve, lambda: V.scalar_tensor_tensor(out=hT[:, fl, tb * 512:(tb + 1) * 512], in0=pa[:, 512:1024],
                                                             scalar=0.5, in1=sg, op0=ALU.mult, op1=ALU.mult),
                         reads=[bpa, bSG[it % 2]], writes=[bh[fl][tb]])
                    it += 1
            if gi + 1 < len(GROUPS):
                load_wo(gi + 1)
            ws = gi % 2
            for tt in range(NT):
                pb = PB[tt % 2]
                bpb = bPB[tt % 2]
                fns = []
                for half in range(2):
                    for fl in range(cnt):
                        fns.append(lambda half=half, fl=fl: P.matmul(
                            pb[:, half * 512:(half + 1) * 512], lhsT=hT[:, fl, tt * 128:(tt + 1) * 128],
                            rhs=WO[ws][:, fl, half * 512:(half + 1) * 512], start=(fl == 0), stop=(fl == cnt - 1)))
                K.mm(fns, reads=[bWO[ws]] + [bh[fl][tt // 4] for fl in range(cnt)], writes=[bpb])
                resid_accum(tt, bpb, pb, first)
                if last:
                    ln_stats(tt)
                    if tt % 4 == 3:
                        ln_finish(tt // 4, final_out=is_final)
                        pend_tr.append(tt // 4)
                    if tt % 4 == 1 and pend_tr:
                        tb0 = pend_tr.pop(0)
                        for j in range(4):
                            transposes_to_XT(4 * tb0 + j, bPA[j % 2], PA[j % 2])
            if last:
                while pend_tr:
                    tb0 = pend_tr.pop(0)
                    for j in range(4):
                        transposes_to_XT(4 * tb0 + j, bPA[j % 2], PA[j % 2])
        sub_count["n"] += 1

    def attn_phase(j, li):
        K.barrier()
        A = Arena()
        WQ = A.take(KC * 1536 * 2, BF16, "p (k n) -> p k n", k=KC)
        WOo = A.take(KC * D * 2, BF16, "p (k n) -> p k n", k=KC)
        GBa = A.take(2 * D * 4, F32, "p (a n) -> p a n", a=2)
        kTa = A.take(3 * 4 * 128 * 2, BF16, "p (t g q) -> p t g q", t=3, g=4)
        Va = A.take(3 * 256 * 2, BF16, "p (t n) -> p t n", t=3)
        COS = A.take(NT * 32 * 4, F32, "p (t n) -> p t n", t=NT)
        SIN = A.take(NT * 32 * 4, F32, "p (t n) -> p t n", t=NT)
        MK = A.take(2 * 256 * 4, F32, "p (a n) -> p a n", a=2)
        SK = A.take(2 * 16 * 4, F32, "p (a n) -> p a n", a=2)
        QKr = A.take(20 * 64 * 2, BF16, "p (h d) -> p h d", h=20)
        KD = A.take(4 * 2 * 64 * 2, BF16, "p (g r d) -> p g r d", g=4, r=2)
        TA = A.take(20 * 32 * 4, F32, "p (h d) -> p h d", h=20)
        TB = A.take(20 * 32 * 4, F32, "p (h d) -> p h d", h=20)
        qT = [A.take(KC * 128 * 2, BF16, "p (c q) -> p c q", c=KC) for _ in range(2)]
        SM = A.take(4 * 256 * 4, F32, "p (h n) -> p h n", h=4)
        PP = A.take(4 * 256 * 2, BF16, "p (h n) -> p h n", h=4)
        PT = A.take(8 * 128 * 2, BF16, "p (a q) -> p a q", a=8)
        OT = [A.take(KC * 128 * 2, BF16, "p (c q) -> p c q", c=KC) for _ in range(2)]
        SMALL = A.take(8 * 4 * 4, F32, "p (a n) -> p a n", a=8)
        HALO = A.take(512 * 4, F32)
        GATH = A.take(8 * 512 * 4, F32, "p (r n) -> p r n", r=8)
        bWQ, bWOo, bCS, bMK, bSK = Buf("WQ"), Buf("WOo"), Buf("CS"), Buf("MK"), Buf("SK")
        bkT = [Buf(f"kT{t}") for t in range(3)]
        bV = [Buf(f"V{t}") for t in range(3)]
        bQKr, bKD, bTA, bTB = Buf("QKr"), Buf("KD"), Buf("TA"), Buf("TB")
        bqT = [Buf("qT0"), Buf("qT1")]
        bSM, bPP, bPT, bSmall = Buf("SM"), Buf("PP"), Buf("PT"), Buf("SMALL")
        bOT = [Buf("OT0"), Buf("OT1")]
        bHALO, bGATH = Buf("HALO"), Buf("GATH")
        R0, R1, R2, R3 = PS[:, 0:1536], PS[:, 1536:2048], PS[:, 2048:3072], PS[:, 3072:4096]
        bR0, bR1, bR2, bR3 = Buf("R0"), Buf("R1"), Buf("R2"), Buf("R3")
        agi = nc.dram_tensor(f"agi_kv{j}", [128, 512], F32)
        ago = nc.dram_tensor(f"ago_kv{j}", [8 * 128, 512], F32)
        bAgi, bAgo = Buf("agi"), Buf("ago")

        def kslot(t):
            if t < 0:
                return 1
            if t == 15:
                return 2
            return t % 2

        load_ln_params(li, 1, GBa)
        K.dma(pool, WQ, wqkv[j].rearrange("(k p) n -> p k n", p=128), [], [bWQ], bWQ)
        K.dma(pool, WOo, wo_d[j].rearrange("(k p) n -> p k n", p=128), [], [bWOo], bWOo)
        K.dma(sp, COS, cos_d.rearrange("(t p) n -> p t n", p=128), [], [bCS], bCS)
        K.dma(sp, SIN, sin_d.rearrange("(t p) n -> p t n", p=128), [], [bCS], bCS)
        K.dma(sp, MK[:, 0, :], mask0_d, [], [bMK], bMK)
        K.dma(sp, MK[:, 1, :], maskn_d, [], [bMK], bMK)
        K.dma(sp, SK[:, 0, :], sinks_d[j:j + 1, :].to_broadcast([128, 16]), [], [bSK], bSK)
        K.op(dve, lambda: V.tensor_scalar(out=SK[:, 1, :], in0=SK[:, 0, :], scalar1=-1.0, scalar2=None, op0=ALU.mult),
             reads=[bSK], writes=[bSK])

        def skv(a, g):
            return SK[:, a, :].rearrange("p (g a2 b) -> p g b a2", g=4, a2=2, b=2)[:, g]

        def sm2(k):
            return SMALL[:, k, :].rearrange("p (b a) -> p b a", b=2)

        def k_transposes(slot):
            r2b = R2.bitcast(BF16)[:, 1024:1536].rearrange("p (g q) -> p g q", g=4)
            K.mm([(lambda g=g: P.transpose(out=r2b[:, g, :], in_=KD[:, g].rearrange("p r d -> p (r d)"), identity=identb))
                  for g in range(4)], reads=[bKD, bConst], writes=[bR2])
            K.op(act, lambda: S.activation(out=kTa[:, slot], in_=r2b, func=AF.Copy), reads=[bR2], writes=[bkT[slot]])

        def prep(tt, qslot):
            ks = kslot(tt)
            fns = []
            for c in range(3):
                for k in range(KC):
                    fns.append(lambda c=c, k=k: P.matmul(R0[:, c * 512:(c + 1) * 512], lhsT=XT[:, k, tt * 128:(tt + 1) * 128],
                                                         rhs=WQ[:, k, c * 512:(c + 1) * 512], start=(k == 0), stop=(k == KC - 1)))
            K.mm(fns, reads=[bWQ, bXT[tt]], writes=[bR0])
            qk = R0[:, 0:1280].rearrange("p (h d) -> p h d", h=20)
            t1, t2 = qk[:, :, 0:32], qk[:, :, 32:64]
            cb = COS[:, tt, :].unsqueeze(1).to_broadcast([128, 20, 32])
            sb = SIN[:, tt, :].unsqueeze(1).to_broadcast([128, 20, 32])
            K.op(dve, lambda: V.tensor_tensor(out=TA, in0=t1, in1=cb, op=ALU.mult), reads=[bR0, bCS], writes=[bTA])
            K.op(dve, lambda: V.tensor_tensor(out=TB, in0=t2, in1=sb, op=ALU.mult), reads=[bR0, bCS], writes=[bTB])
            K.op(dve, lambda: V.tensor_tensor(out=QKr[:, :, 0:32], in0=TA, in1=TB, op=ALU.subtract), reads=[bTA, bTB], writes=[bQKr])
            K.op(dve, lambda: V.tensor_tensor(out=TA, in0=t2, in1=cb, op=ALU.mult), reads=[bR0, bCS], writes=[bTA])
            K.op(dve, lambda: V.tensor_tensor(out=TB, in0=t1, in1=sb, op=ALU.mult), reads=[bR0, bCS], writes=[bTB])
            K.op(dve, lambda: V.tensor_tensor(out=QKr[:, :, 32:64], in0=TA, in1=TB, op=ALU.add), reads=[bTA, bTB], writes=[bQKr])
            K.op(act, lambda: S.activation(out=Va[:, ks, :], in_=R0[:, 1280:1536], func=AF.Copy), reads=[bR0], writes=[bV[ks]])
            for r in range(2):
                K.op(dve, lambda r=r: V.tensor_copy(out=KD[:, :, r, :], in_=QKr[:, 16:20, :]), reads=[bQKr], writes=[bKD])
            r2q = R2.bitcast(BF16)[:, 0:1024].rearrange("p (c q) -> p c q", c=KC)
            K.mm([(lambda c=c: P.transpose(out=r2q[:, c, :], in_=QKr[:, 2 * c:2 * c + 2, :].rearrange("p h d -> p (h d)"), identity=identb))
                  for c in range(KC)], reads=[bQKr, bConst], writes=[bR2])
            K.op(act, lambda: S.activation(out=qT[qslot], in_=r2q, func=AF.Copy), reads=[bR2], writes=[bqT[qslot]])
            k_transposes(ks)

        def attend(qb, qslot, oslot):
            mk = MK[:, 0 if qb == 0 else 1, :]
            r2f = R2.rearrange("p (c q) -> p c q", c=KC)
            kss = [kslot(qb - 1), kslot(qb)]
            for g in range(4):
                s3 = R3.rearrange("p (h n) -> p h n", h=4)
                fns = []
                for sl in range(4):
                    b_, a2 = sl // 2, sl % 2
                    h = 4 * g + 2 * a2 + b_
                    pb0 = b_ * 64
                    for kb in range(2):
                        fns.append(lambda sl=sl, h=h, pb0=pb0, kb=kb: P.matmul(
                            s3[:, sl, kb * 128:(kb + 1) * 128], lhsT=qT[qslot][pb0:pb0 + 64, h // 2, :],
                            rhs=kTa[pb0:pb0 + 64, kss[kb], g, :], start=True, stop=True))
                K.mm(fns, reads=[bqT[qslot], bkT[kss[0]], bkT[kss[1]]], writes=[bR3])
                K.op(dve, lambda: V.tensor_tensor(out=SM, in0=s3, in1=mk.unsqueeze(1).to_broadcast([128, 4, 256]), op=ALU.add),
                     reads=[bR3, bMK], writes=[bSM])
                K.op(dve, lambda: V.tensor_reduce(out=SMALL[:, 0, :], in_=SM, axis=AX.X, op=ALU.max), reads=[bSM], writes=[bSmall])
                K.op(dve, lambda: V.scalar_tensor_tensor(out=sm2(1), in0=sm2(0), scalar=-0.125,
                                                         in1=skv(1, g), op0=ALU.mult, op1=ALU.min),
                     reads=[bSmall, bSK], writes=[bSmall])
                for sl in range(4):
                    K.op(act, lambda sl=sl: S.activation(out=PP[:, sl, :], in_=SM[:, sl, :], func=AF.Exp, bias=SMALL[:, 1, sl:sl + 1],
                                                         scale=0.125, accum_out=SMALL[:, 2, sl:sl + 1]),
                         reads=[bSM, bSmall], writes=[bPP, bSmall])
                K.op(dve, lambda: V.tensor_tensor(out=sm2(3), in0=sm2(1), in1=skv(0, g), op=ALU.add),
                     reads=[bSmall, bSK], writes=[bSmall])
                K.op(act, lambda: S.activation(out=SMALL[:, 4, :], in_=SMALL[:, 3, :], func=AF.Exp), reads=[bSmall], writes=[bSmall])
                K.op(dve, lambda: V.tensor_tensor(out=SMALL[:, 5, :], in0=SMALL[:, 2, :], in1=SMALL[:, 4, :], op=ALU.add),
                     reads=[bSmall], writes=[bSmall])
                K.op(dve, lambda: V.reciprocal(out=SMALL[:, 6, :], in_=SMALL[:, 5, :]), reads=[bSmall], writes=[bSmall])
                K.op(dve, lambda: V.tensor_tensor(out=PP, in0=PP, in1=SMALL[:, 6, :].unsqueeze(2).to_broadcast([128, 4, 256]), op=ALU.mult),
                     reads=[bSmall], writes=[bPP])
                r1b = R1.bitcast(BF16).rearrange("p (a q) -> p a q", a=8)
                K.mm([(lambda sl=sl, kb=kb: P.transpose(out=r1b[:, 2 * sl + kb, :], in_=PP[:, sl, kb * 128:(kb + 1) * 128], identity=identb))
                      for sl in range(4) for kb in range(2)], reads=[bPP, bConst], writes=[bR1])
                K.op(act, lambda: S.activation(out=PT, in_=r1b, func=AF.Copy), reads=[bR1], writes=[bPT])
                fns = []
                for sl in range(4):
                    b_, a2 = sl // 2, sl % 2
                    h = 4 * g + 2 * a2 + b_
                    pb0 = b_ * 64
                    for kb in range(2):
                        fns.append(lambda sl=sl, h=h, pb0=pb0, kb=kb: P.matmul(
                            r2f[pb0:pb0 + 64, h // 2, :], lhsT=Va[:, kss[kb], g * 64:(g + 1) * 64],
                            rhs=PT[:, 2 * sl + kb, :], start=(kb == 0), stop=(kb == 1)))
                K.mm(fns, reads=[bPT, bV[kss[0]], bV[kss[1]]], writes=[bR2])
            K.op(act, lambda: S.activation(out=OT[oslot], in_=r2f, func=AF.Copy), reads=[bR2], writes=[bOT[oslot]])
            y = R0[:, 0:1024]
            fns = []
            for half in range(2):
                for c in range(KC):
                    fns.append(lambda half=half, c=c: P.matmul(y[:, half * 512:(half + 1) * 512], lhsT=OT[oslot][:, c, :],
                                                               rhs=WOo[:, c, half * 512:(half + 1) * 512], start=(c == 0), stop=(c == KC - 1)))
            K.mm(fns, reads=[bOT[oslot], bWOo], writes=[bR0])
            resid_accum(qb, bR0, y, True)
            ln_stats(qb)

        prep(15, 1)
        K.op(dve, lambda: V.tensor_copy(out=HALO[:, 0:256], in_=QKr[:, 16:20, :].rearrange("p h d -> p (h d)")),
             reads=[bQKr], writes=[bHALO])
        K.op(dve, lambda: V.tensor_copy(out=HALO[:, 256:512], in_=Va[:, 2, :]), reads=[bV[2]], writes=[bHALO])
        K.dma(pool, agi.ap(), HALO, [bHALO], [bAgi], bAgi)
        K._pre(pool, [bAgi], [bAgo])
        cc = G.collective_compute("AllGather", ALU.bypass, replica_groups=[list(range(NCORES))],
                                  ins=[agi.ap().opt()], outs=[ago.ap().opt()])
        K._post(pool.mark(cc), [bAgi], [bAgo])
        K.dma(pool, GATH, ago.ap().rearrange("(r p) n -> p r n", p=128), [bAgo], [bGATH], bGATH)
        K.op(dve, lambda: V.tensor_scalar(out=HALO, in0=GATH[:, 0, :], scalar1=selt[:, 0:1], scalar2=None, op0=ALU.mult),
             reads=[bGATH, bSel], writes=[bHALO])
        for r in range(1, 8):
            K.op(dve, lambda r=r: V.scalar_tensor_tensor(out=HALO, in0=GATH[:, r, :], scalar=selt[:, r:r + 1], in1=HALO,
                                                         op0=ALU.mult, op1=ALU.add), reads=[bGATH, bSel], writes=[bHALO])
        for r in range(2):
            K.op(dve, lambda r=r: V.tensor_copy(out=KD[:, :, r, :], in_=HALO[:, 0:256].rearrange("p (g d) -> p g d", g=4)),
                 reads=[bHALO], writes=[bKD])
        K.op(dve, lambda: V.tensor_copy(out=Va[:, 1, :], in_=HALO[:, 256:512]), reads=[bHALO], writes=[bV[1]])
        k_transposes(1)
        for qb in range(NT):
            if qb < 15:
                prep(qb, 0)
                attend(qb, 0, qb % 2)
            else:
                attend(qb, 1, qb % 2)
            if qb % 4 == 3:
                ln_finish(qb // 4)
                for jj in range(4):
                    transposes_to_XT(4 * (qb // 4) + jj, bR3, R3)
        sub_count["n"] += 1

    def lru_full(j, li):
        K.barrier()
        A = Arena()
        G1 = A.take(KC * T * 2, BF16, "p (c t) -> p c t", c=KC)
        G2 = A.take(KC * T * 2, BF16, "p (c t) -> p c t", c=KC)
        wreg0 = A.off
        WX = A.take(KC * 256 * 2, BF16, "p (k n) -> p k n", k=KC)
        WGt = A.take(KC * 256 * 2, BF16, "p (k n) -> p k n", k=KC)
        treg0 = A.off
        T2 = A.take(2 * 512 * 4, F32, "p (c t) -> p c t", c=2)
        T4 = A.take(2 * 512 * 4, F32, "p (c t) -> p c t", c=2)
        T5 = A.take(2 * 512 * 4, F32, "p (c t) -> p c t", c=2)
        T1 = A.take(2 * 512 * 4, F32, "p (c t) -> p c t", c=2)
        WRA = [A.take(2 * 256 * 2, BF16, "p (c n) -> p c n", c=2) for _ in range(2)]
        WRX = [A.take(2 * 256 * 2, BF16, "p (c n) -> p c n", c=2) for _ in range(2)]
        XB = A.take(2 * 516 * 4, F32, "p (c t) -> p c t", c=2)
        T3 = A.take(2 * 512 * 2, BF16, "p (c t) -> p c t", c=2)
        ZER = A.take(512 * 2, BF16)
        PR = A.take(12 * 8 * 4, F32, "p (a c) -> p a c", a=12)
        XTH = A.take(KC * 4 * 2, BF16, "p (k t) -> p k t", k=KC)
        HX = A.take(32 * 4, F32)
        GX = A.take(8 * 32 * 4, F32, "p (r n) -> p r n", r=8)
        CAR = A.take(16 * 4, F32)
        GC = A.take(8 * 16 * 4, F32, "p (r n) -> p r n", r=8)
        CIN = A.take(8 * 4, F32)
        CT = A.take(8 * 4, F32)
        HL = A.take(2 * 4 * 4, F32, "p (c t) -> p c t", c=2)
        bWX, bWG, bWRA, bWRX = Buf("WX"), Buf("WG"), [Buf(), Buf()], [Buf(), Buf()]
        bXB, bT1, bT2, bT3, bT4, bT5, bZ, bPR = Buf("XB"), Buf("T1"), Buf("T2"), Buf("T3"), Buf("T4"), Buf("T5"), Buf("Z"), Buf("PR")
        bXTH, bHX, bGX, bCAR, bGC, bCIN, bHL = Buf(), Buf(), Buf(), Buf(), Buf(), Buf(), Buf()
        bG = [[Buf(f"G{c}_{tb}") for tb in range(4)] for c in range(KC)]
        L0, L1, L2, L3 = PS[:, 0:1024], PS[:, 1024:2048], PS[:, 2048:3072], PS[:, 3072:4096]
        bL0, bL1, bL2, bL3 = Buf("L0"), Buf("L1"), Buf("L2"), Buf("L3")
        agx_i = nc.dram_tensor(f"agx_i{j}", [128, 32], F32)
        agx_o = nc.dram_tensor(f"agx_o{j}", [8 * 128, 32], F32)
        agc_i = nc.dram_tensor(f"agc_i{j}", [128, 16], F32)
        agc_o = nc.dram_tensor(f"agc_o{j}", [8 * 128, 16], F32)
        bAxi, bAxo, bAci, bAco = Buf(), Buf(), Buf(), Buf()

        K.op(dve, lambda: V.memset(HX, 0.0), writes=[bHX])
        K.op(dve, lambda: V.tensor_copy(out=HX[:, 0:24].rearrange("p (k t) -> p k t", k=KC), in_=XT[:, :, T - 3:T]),
             reads=[bXT[15]], writes=[bHX])
        K.dma(pool, agx_i.ap(), HX, [bHX], [bAxi], bAxi)
        K._pre(pool, [bAxi], [bAxo])
        cc = G.collective_compute("AllGather", ALU.bypass, replica_groups=[list(range(NCORES))],
                                  ins=[agx_i.ap().opt()], outs=[agx_o.ap().opt()])
        K._post(pool.mark(cc), [bAxi], [bAxo])
        K.dma(pool, GX, agx_o.ap().rearrange("(r p) n -> p r n", p=128), [bAxo], [bGX], bGX)
        with nc.allow_non_contiguous_dma(reason="tiny per-channel parameter vectors"):
            for w in range(4):
                K.dma(sp, PR[:, w, :], lcw[j, w].rearrange("(c p) -> p c", p=128), [], [bPR], bPR)
            K.dma(sp, PR[:, 4, :], lcb[j].rearrange("(c p) -> p c", p=128), [], [bPR], bPR)
            K.dma(sp, PR[:, 5, :], lbra[j].rearrange("(c p) -> p c", p=128), [], [bPR], bPR)
            K.dma(sp, PR[:, 6, :], lbrx[j].rearrange("(c p) -> p c", p=128), [], [bPR], bPR)
            K.dma(sp, PR[:, 7, :], llam[j].rearrange("(c p) -> p c", p=128), [], [bPR], bPR)
        K.op(act, lambda: S.activation(out=PR[:, 10, :], in_=PR[:, 7, :], func=AF.Exp, scale=-1.0), reads=[bPR], writes=[bPR])
        K.op(act, lambda: S.activation(out=PR[:, 11, :], in_=PR[:, 10, :], func=AF.Ln, bias=onec, scale=1.0), reads=[bPR, bConst], writes=[bPR])
        K.op(dve, lambda: V.tensor_scalar(out=PR[:, 8, :], in0=PR[:, 11, :], scalar1=-8.0, scalar2=None, op0=ALU.mult), reads=[bPR], writes=[bPR])
        K.op(dve, lambda: V.tensor_scalar(out=PR[:, 9, :], in0=PR[:, 11, :], scalar1=-16.0, scalar2=None, op0=ALU.mult), reads=[bPR], writes=[bPR])
        K.op(dve, lambda: V.memset(ZER, 0.0), writes=[bZ])
        K.op(dve, lambda: V.tensor_scalar(out=HX, in0=GX[:, 0, :], scalar1=selt[:, 0:1], scalar2=None, op0=ALU.mult),
             reads=[bGX, bSel], writes=[bHX])
        for r in range(1, 8):
            K.op(dve, lambda r=r: V.scalar_tensor_tensor(out=HX, in0=GX[:, r, :], scalar=selt[:, r:r + 1], in1=HX,
                                                         op0=ALU.mult, op1=ALU.add), reads=[bGX, bSel], writes=[bHX])
        K.op(dve, lambda: V.memset(XTH, 0.0), writes=[bXTH])
        K.op(dve, lambda: V.tensor_copy(out=XTH[:, :, 0:3], in_=HX[:, 0:24].rearrange("p (k t) -> p k t", k=KC)),
             reads=[bHX], writes=[bXTH])

        wv = lwin[j].rearrange("(k p) n -> p k n", p=128)

        def load_w(n):
            s = n % 2
            K.dma(pool, WX, wv[:, :, n * 256:(n + 1) * 256], [], [bWX], bWX)
            K.dma(pool, WGt, wv[:, :, 1024 + n * 256:1024 + (n + 1) * 256], [], [bWG], bWG)
            K.dma(pool, WRA[s], lwra[j, n].rearrange("(c p) o -> p c o", p=128), [], [bWRA[s]], bWRA[s])
            K.dma(pool, WRX[s], lwrx[j, n].rearrange("(c p) o -> p c o", p=128), [], [bWRX[s]], bWRX[s])

        for n in range(4):
            load_w(n)
            s = n % 2
            for tb in range(4):
                rhs_b = [bXT[4 * tb + q] for q in range(4)]
                l0 = L0.rearrange("p (c t) -> p c t", c=2)
                fns = []
                for c in range(2):
                    for k in range(KC):
                        fns.append(lambda c=c, k=k: P.matmul(l0[:, c, :], lhsT=WX[:, k, c * 128:(c + 1) * 128],
                                                             rhs=XT[:, k, tb * 512:(tb + 1) * 512], start=(k == 0), stop=(k == KC - 1)))
                K.mm(fns, reads=[bWX] + rhs_b, writes=[bL0])
                if tb == 0:
                    l3h = L3[:, 0:8].rearrange("p (c t) -> p c t", c=2)
                    fns = []
                    for c in range(2):
                        for k in range(KC):
                            fns.append(lambda c=c, k=k: P.matmul(l3h[:, c, :], lhsT=WX[:, k, c * 128:(c + 1) * 128],
                                                                 rhs=XTH[:, k, :], start=(k == 0), stop=(k == KC - 1)))
                    K.mm(fns, reads=[bWX, bXTH], writes=[bL3])
                    K.op(dve, lambda: V.tensor_copy(out=XB[:, :, 0:3], in_=l3h[:, :, 0:3]), reads=[bL3], writes=[bXB])
                else:
                    K.op(dve, lambda: V.tensor_copy(out=XB[:, :, 0:3], in_=XB[:, :, 512:515]), reads=[], writes=[bXB])
                K.op(act, lambda: S.activation(out=XB[:, :, 3:515], in_=l0, func=AF.Copy), reads=[bL0], writes=[bXB])
                for c in range(2):
                    ch = 2 * n + c
                    K.op(dve, lambda c=c, ch=ch: V.tensor_scalar(out=T2[:, c, :], in0=XB[:, c, 3:515], scalar1=PR[:, 3, ch:ch + 1],
                                                                 scalar2=PR[:, 4, ch:ch + 1], op0=ALU.mult, op1=ALU.add),
                         reads=[bXB, bPR], writes=[bT2])
                    for w in range(3):
                        K.op(dve, lambda c=c, ch=ch, w=w: V.scalar_tensor_tensor(
                            out=T2[:, c, :], in0=XB[:, c, w:w + 512], scalar=PR[:, w, ch:ch + 1], in1=T2[:, c, :],
                            op0=ALU.mult, op1=ALU.add), reads=[bXB, bPR], writes=[bT2])
                K.op(dve, lambda: V.tensor_copy(out=T3, in_=T2), reads=[bT2], writes=[bT3])
                l1 = L1.rearrange("p (c t) -> p c t", c=2)
                l2 = L2.rearrange("p (c t) -> p c t", c=2)
                for (lw, bw, lps, bps) in ((WRA[s], bWRA[s], l1, bL1), (WRX[s], bWRX[s], l2, bL2)):
                    fns = []
                    for jo in range(2):
                        for ci in range(2):
                            fns.append(lambda jo=jo, ci=ci, lw=lw, lps=lps: P.matmul(
                                lps[:, jo, :], lhsT=lw[:, ci, jo * 128:(jo + 1) * 128], rhs=T3[:, ci, :],
                                start=(ci == 0), stop=(ci == 1)))
                    K.mm(fns, reads=[bw, bT3], writes=[bps])
                for c in range(2):
                    ch = 2 * n + c
                    K.op(act, lambda c=c, ch=ch: S.activation(out=T4[:, c, :], in_=l1[:, c, :], func=AF.Sigmoid,
                                                              bias=PR[:, 5, ch:ch + 1], scale=1.0), reads=[bL1, bPR], writes=[bT4])
                for c in range(2):
                    ch = 2 * n + c
                    K.op(act, lambda c=c, ch=ch: S.activation(out=T1[:, c, :], in_=l2[:, c, :], func=AF.Sigmoid,
                                                              bias=PR[:, 6, ch:ch + 1], scale=1.0), reads=[bL2, bPR], writes=[bT1])
                for c in range(2):
                    ch = 2 * n + c
                    K.op(act, lambda c=c, ch=ch: S.activation(out=T5[:, c, :], in_=T4[:, c, :], func=AF.Exp,
                                                              scale=PR[:, 8, ch:ch + 1]), reads=[bT4, bPR], writes=[bT5])
                for c in range(2):
                    ch = 2 * n + c
                    K.op(act, lambda c=c, ch=ch: S.activation(out=T4[:, c, :], in_=T4[:, c, :], func=AF.Exp,
                                                              scale=PR[:, 9, ch:ch + 1]), reads=[bPR], writes=[bT4])
                K.op(act, lambda: S.activation(out=T4, in_=T4, func=AF.Sqrt, bias=onec, scale=-1.0), reads=[bConst], writes=[bT4])
                K.op(dve, lambda: V.tensor_tensor(out=T1, in0=T1, in1=T2, op=ALU.mult), reads=[bT2], writes=[bT1])
                K.op(dve, lambda: V.tensor_tensor(out=T1, in0=T1, in1=T4, op=ALU.mult), reads=[bT4], writes=[bT1])
                for c in range(2):
                    init_h = 0.0 if tb == 0 else HL[:, c, 0:1]
                    init_a = 1.0 if tb == 0 else HL[:, c, 1:2]
                    K.op(dve, lambda c=c, init_h=init_h: V.tensor_tensor_scan(out=T2[:, c, :], data0=T5[:, c, :], data1=T1[:, c, :],
                                                                              initial=init_h, op0=ALU.mult, op1=ALU.add),
                         reads=[bT5, bT1, bHL], writes=[bT2])
                    K.op(dve, lambda c=c, init_a=init_a: V.tensor_tensor_scan(out=T4[:, c, :], data0=T5[:, c, :], data1=ZER,
                                                                              initial=init_a, op0=ALU.mult, op1=ALU.add),
                         reads=[bT5, bZ, bHL], writes=[bT4])
                for c in range(2):
                    K.op(dve, lambda c=c: V.tensor_copy(out=HL[:, c, 0:1], in_=T2[:, c, 511:512]), reads=[bT2], writes=[bHL])
                    K.op(dve, lambda c=c: V.tensor_copy(out=HL[:, c, 1:2], in_=T4[:, c, 511:512]), reads=[bT4], writes=[bHL])
                if tb == 3:
                    for c in range(2):
                        ch = 2 * n + c
                        K.op(dve, lambda c=c, ch=ch: V.tensor_copy(out=CAR[:, ch:ch + 1], in_=T4[:, c, 511:512]), reads=[bT4], writes=[bCAR])
                        K.op(dve, lambda c=c, ch=ch: V.tensor_copy(out=CAR[:, 8 + ch:9 + ch], in_=T2[:, c, 511:512]), reads=[bT2], writes=[bCAR])
                l3 = L3.rearrange("p (c t) -> p c t", c=2)
                fns = []
                for c in range(2):
                    for k in range(KC):
                        fns.append(lambda c=c, k=k: P.matmul(l3[:, c, :], lhsT=WGt[:, k, c * 128:(c + 1) * 128],
                                                             rhs=XT[:, k, tb * 512:(tb + 1) * 512], start=(k == 0), stop=(k == KC - 1)))
                K.mm(fns, reads=[bWG] + rhs_b, writes=[bL3])
                K.op(act, lambda: S.activation(out=T1, in_=l3, func=AF.Gelu_apprx_tanh), reads=[bL3], writes=[bT1])
                for c in range(2):
                    ch = 2 * n + c
                    K.op(dve, lambda c=c, ch=ch: V.tensor_tensor(out=G1[:, ch, tb * 512:(tb + 1) * 512], in0=T2[:, c, :], in1=T1[:, c, :], op=ALU.mult),
                         reads=[bT2, bT1], writes=[bG[ch][tb]])
                    K.op(dve, lambda c=c, ch=ch: V.tensor_tensor(out=G2[:, ch, tb * 512:(tb + 1) * 512], in0=T4[:, c, :], in1=T1[:, c, :], op=ALU.mult),
                         reads=[bT4, bT1], writes=[bG[ch][tb]])
        K.dma(pool, agc_i.ap(), CAR, [bCAR], [bAci], bAci)
        K._pre(pool, [bAci], [bAco])
        cc = G.collective_compute("AllGather", ALU.bypass, replica_groups=[list(range(NCORES))],
                                  ins=[agc_i.ap().opt()], outs=[agc_o.ap().opt()])
        K._post(pool.mark(cc), [bAci], [bAco])
        K.dma(pool, GC, agc_o.ap().rearrange("(r p) n -> p r n", p=128), [bAco], [bGC], bGC)
        K.barrier()
        WOl = arena[:, treg0:treg0 + 4096].bitcast(BF16).rearrange("p (k n) -> p k n", k=KC)
        GBa = arena[:, wreg0:wreg0 + 2048].rearrange("p (a n) -> p a n", a=2)
        bWOl = Buf("WOl")
        K.dma(pool, WOl, lwout[j].rearrange("(k p) n -> p k n", p=128), [], [bWOl], bWOl)
        load_ln_params(li, 1, GBa)
        K.op(dve, lambda: V.memset(CIN, 0.0), writes=[bCIN])
        for r in range(8):
            K.op(dve, lambda r=r: V.tensor_scalar(out=CT, in0=GC[:, r, 0:8], scalar1=-1.0, scalar2=wselt[:, r:r + 1],
                                                  op0=ALU.add, op1=ALU.mult), reads=[bGC, bWsel], writes=[bCIN])
            K.op(dve, lambda: V.scalar_tensor_tensor(out=CT, in0=CT, scalar=1.0, in1=CIN, op0=ALU.add, op1=ALU.mult),
                 reads=[], writes=[bCIN])
            K.op(dve, lambda r=r: V.scalar_tensor_tensor(out=CIN, in0=GC[:, r, 8:16], scalar=wselt[:, r:r + 1], in1=CT,
                                                         op0=ALU.mult, op1=ALU.add), reads=[bGC, bWsel], writes=[bCIN])
        for tb in range(4):
            for ch in range(KC):
                K.op(dve, lambda ch=ch, tb=tb: V.scalar_tensor_tensor(
                    out=G1[:, ch, tb * 512:(tb + 1) * 512], in0=G2[:, ch, tb * 512:(tb + 1) * 512], scalar=CIN[:, ch:ch + 1],
                    in1=G1[:, ch, tb * 512:(tb + 1) * 512], op0=ALU.mult, op1=ALU.add), reads=[bCIN], writes=[bG[ch][tb]])
        bPB = [Buf("PB0"), Buf("PB1")]
        PB = [PS[:, 2048:3072], PS[:, 3072:4096]]
        bPA = [Buf("PA0"), Buf("PA1")]
        PA = [PS[:, 0:1024], PS[:, 1024:2048]]
        pend = []
        for tt in range(NT):
            pb, bpb = PB[tt % 2], bPB[tt % 2]
            fns = []
            for half in range(2):
                for c in range(KC):
                    fns.append(lambda half=half, c=c: P.matmul(pb[:, half * 512:(half + 1) * 512], lhsT=G1[:, c, tt * 128:(tt + 1) * 128],
                                                               rhs=WOl[:, c, half * 512:(half + 1) * 512], start=(c == 0), stop=(c == KC - 1)))
            K.mm(fns, reads=[bWOl] + [bG[c][tt // 4] for c in range(KC)], writes=[bpb])
            resid_accum(tt, bpb, pb, True)
            ln_stats(tt)
            if tt % 4 == 3:
                ln_finish(tt // 4)
                pend.append(tt // 4)
            if tt % 4 == 1 and pend:
                tb0 = pend.pop(0)
                for jj in range(4):
                    transposes_to_XT(4 * tb0 + jj, bPA[jj % 2], PA[jj % 2])
        while pend:
            tb0 = pend.pop(0)
            for jj in range(4):
                transposes_to_XT(4 * tb0 + jj, bPA[jj % 2], PA[jj % 2])
        sub_count["n"] += 1

    nl = len(layers)
    for idx, i in enumerate(layers):
        last_layer = idx == nl - 1
        if done():
            break
        ffn_phase(w1i[i], w1o[i], i, 0, is_final=(n_sub is not None and sub_count["n"] + 1 == n_sub))
        if done():
            break
        if i % 2 == 0:
            attn_phase(i // 2, i)
        else:
            lru_full(i // 2, i)
        if done():
            break
        fin = last_layer or (n_sub is not None and sub_count["n"] + 1 == n_sub)
        ffn_phase(w2i[i], w2o[i], i, 2, is_final=fin)
    if n_sub is not None and bOut.dsem is None:
        K.barrier()
        for tt in range(NT):
            K.dma(sp, out_d.rearrange("(t p) d -> p t d", p=128)[:, tt, :], X[:, tt, :], [bX[tt]], [], bOut)
    sp.wait(Tok(bOut.dsem, bOut.dcnt, "dma"))
    return nc


def host_tables(core):
    b, q = core // 4, core % 4
    pos = (np.arange(T, dtype=np.float32) + np.float32(q * T)).astype(np.float32)
    inv_freq = (np.float32(10000.0) ** (-np.arange(0, 64, 2, dtype=np.float32) / np.float32(64))).astype(np.float32)
    ang = (pos[:, None] * inv_freq[None, :]).astype(np.float32)
    cos_t, sin_t = np.cos(ang).astype(np.float32), np.sin(ang).astype(np.float32)
    i = np.arange(128)[:, None]
    jj = np.arange(128)[None, :]
    prev = np.where(jj > i, 0.0, NEG).astype(np.float32)
    cur = np.where(jj <= i, 0.0, NEG).astype(np.float32)
    maskn = np.concatenate([prev, cur], axis=1)
    mask0 = maskn.copy()
    if q == 0:
        mask0[:, 0:128] = NEG
    sel = np.zeros((128, 8), np.float32)
    if q > 0:
        sel[:, core - 1] = 1.0
    wsel = np.zeros((128, 8), np.float32)
    for r in range(b * 4, core):
        wsel[:, r] = 1.0
    return dict(cos_t=cos_t, sin_t=sin_t, mask0=mask0, maskn=maskn, sel=sel, wsel=wsel)


_W_NAMES = ["ffn1_w_in", "ffn1_w_out", "ffn2_w_in", "ffn2_w_out", "ln_g", "ln_b", "attn_w_qkv", "attn_sinks", "attn_w_o",
            "lru_w_in", "lru_conv_w", "lru_conv_b", "lru_w_ra", "lru_b_ra", "lru_w_rx", "lru_b_rx", "lru_lambda", "lru_w_out"]


def run(inputs, layers=(0, 1, 2, 3), n_sub=None):
    x = np.ascontiguousarray(np.asarray(inputs["x"], dtype=np.float32)).reshape(NCORES, T, D)
    shared = {k: np.ascontiguousarray(np.asarray(inputs[k], dtype=np.float32)) for k in _W_NAMES}
    nc = build_program(layers, n_sub)
    in_maps = []
    for c in range(NCORES):
        m = dict(shared)
        m["x"] = x[c]
        m.update(host_tables(c))
        in_maps.append(m)
    res = run_bass_kernel_spmd(nc, in_maps, core_ids=list(range(NCORES)))
    out = np.stack([np.asarray(res.results[c]["out"]) for c in range(NCORES)], axis=0)
    return out.reshape(2, 4 * T, D).astype(np.float32)


def kernel(**inputs):
    return run(inputs)
```

```python
import numpy as np
import concourse.bass as bass
import concourse.mybir as mybir
from concourse.bass_utils import run_bass_kernel_spmd

F32 = mybir.dt.float32
BF16 = mybir.dt.bfloat16
AF = mybir.ActivationFunctionType
ALU = mybir.AluOpType
AX = mybir.AxisListType

NCORES = 8
T = 2048
NT = 16
D = 1024
KC = 8
DFF = 2816
NF = 22
DEPTH = 4
ALPHA = (2.0 * DEPTH) ** 0.25
EPS = 1e-5
NEG = -30000.0
GROUPS = [(0, 8), (8, 7), (15, 7)]
GMAX = 8


class Tok:
    __slots__ = ("sem", "cnt", "own")

    def __init__(self, sem, cnt, own):
        self.sem, self.cnt, self.own = sem, cnt, own


class Buf:
    __slots__ = ("w", "r", "dsem", "dcnt", "name", "keep")

    def __init__(self, name="", keep=False):
        self.w = None
        self.r = {}
        self.dsem = None
        self.dcnt = 0
        self.name = name
        self.keep = keep


class Eng:
    def __init__(self, nc, name, h, selfsync):
        self.nc, self.name, self.h, self.selfsync = nc, name, h, selfsync
        self.gen = 0
        self.sem = nc.alloc_semaphore(f"{name}_p{self.gen}")
        self.cnt = 0
        self.seen = {}

    def wait(self, tok):
        if tok is None:
            return
        if tok.own == self.name and not self.selfsync:
            return
        k = id(tok.sem)
        if self.seen.get(k, 0) >= tok.cnt:
            return
        self.h.wait_ge(tok.sem, tok.cnt)
        self.seen[k] = tok.cnt

    def mark(self, ins):
        if self.cnt >= 30000:
            self.gen += 1
            self.sem = self.nc.alloc_semaphore(f"{self.name}_p{self.gen}")
            self.cnt = 0
        self.cnt += 1
        ins.then_inc(self.sem, 1)
        return Tok(self.sem, self.cnt, self.name)

    def cur(self):
        return Tok(self.sem, self.cnt, self.name) if self.cnt > 0 else None


class Ctx:
    def __init__(self, nc):
        self.nc = nc
        self.pe = Eng(nc, "pe", nc.tensor, False)
        self.act = Eng(nc, "act", nc.scalar, True)
        self.dve = Eng(nc, "dve", nc.vector, True)
        self.pool = Eng(nc, "pool", nc.gpsimd, True)
        self.sp = Eng(nc, "sp", nc.sync, False)
        self.engs = [self.pe, self.act, self.dve, self.pool, self.sp]
        self.nsem = 0
        self.free_dsems = []
        self.phase_dbufs = []

    def _pre(self, eng, reads, writes):
        for b in reads:
            eng.wait(b.w)
        for b in writes:
            eng.wait(b.w)
            for t in b.r.values():
                eng.wait(t)

    def _post(self, tok, reads, writes):
        for b in reads:
            b.r[id(tok.sem)] = tok
        for b in writes:
            b.w = tok
            b.r = {}

    def op(self, eng, fn, reads=(), writes=()):
        self._pre(eng, reads, writes)
        ins = fn()
        tok = eng.mark(ins)
        self._post(tok, reads, writes)
        return ins

    def mm(self, fns, reads, writes):
        eng = self.pe
        self._pre(eng, reads, writes)
        ins = None
        for fn in fns:
            ins = fn()
        tok = eng.mark(ins)
        self._post(tok, reads, writes)

    def dma(self, q, out, in_, reads, writes, sbuf):
        self._pre(q, reads, writes)
        ins = q.h.dma_start(out=out, in_=in_)
        if sbuf.dsem is None:
            if self.free_dsems:
                sbuf.dsem, sbuf.dcnt = self.free_dsems.pop()
            else:
                sbuf.dsem = self.nc.alloc_semaphore(f"dma{self.nsem}")
                self.nsem += 1
            if not sbuf.keep:
                self.phase_dbufs.append(sbuf)
        sbuf.dcnt += 16
        ins.then_inc(sbuf.dsem, 16)
        tok = Tok(sbuf.dsem, sbuf.dcnt, "dma")
        self._post(tok, reads, writes)
        return tok

    def barrier(self):
        toks = [e.cur() for e in self.engs]
        for e in self.engs:
            for t in toks:
                if t is not None and t.own != e.name:
                    e.wait(t)
        for b in self.phase_dbufs:
            t = Tok(b.dsem, b.dcnt, "dma")
            self.pool.wait(t)
            self.sp.wait(t)
            self.free_dsems.append((b.dsem, b.dcnt))
            b.dsem = None
        self.phase_dbufs = []


def build_program(layers=(0, 1, 2, 3), n_sub=None):
    nc = bass.Bass("TRN2", target_bir_lowering=False)
    K = Ctx(nc)
    pe, act, dve, pool, sp = K.pe, K.act, K.dve, K.pool, K.sp
    V, S, P, G = nc.vector, nc.scalar, nc.tensor, nc.gpsimd

    def din(name, shape):
        return nc.dram_tensor(name, list(shape), F32, kind="ExternalInput").ap()

    x_d = din("x", [T, D])
    w1i = din("ffn1_w_in", [DEPTH, D, 2 * DFF])
    w1o = din("ffn1_w_out", [DEPTH, DFF, D])
    w2i = din("ffn2_w_in", [DEPTH, D, 2 * DFF])
    w2o = din("ffn2_w_out", [DEPTH, DFF, D])
    lng = din("ln_g", [DEPTH, 3, D])
    lnb = din("ln_b", [DEPTH, 3, D])
    wqkv = din("attn_w_qkv", [2, D, 1536])
    sinks_d = din("attn_sinks", [2, 16])
    wo_d = din("attn_w_o", [2, D, D])
    lwin = din("lru_w_in", [2, D, 2048])
    lcw = din("lru_conv_w", [2, 4, D])
    lcb = din("lru_conv_b", [2, D])
    lwra = din("lru_w_ra", [2, 4, 256, 256])
    lbra = din("lru_b_ra", [2, D])
    lwrx = din("lru_w_rx", [2, 4, 256, 256])
    lbrx = din("lru_b_rx", [2, D])
    llam = din("lru_lambda", [2, D])
    lwout = din("lru_w_out", [2, D, D])
    cos_d = din("cos_t", [T, 32])
    sin_d = din("sin_t", [T, 32])
    mask0_d = din("mask0", [128, 256])
    maskn_d = din("maskn", [128, 256])
    sel_d = din("sel", [128, 8])
    wsel_d = din("wsel", [128, 8])
    out_d = nc.dram_tensor("out", [T, D], F32, kind="ExternalOutput").ap()

    X = nc.alloc_sbuf_tensor("X", [128, NT, D], F32).ap()
    XT = nc.alloc_sbuf_tensor("XT", [128, KC, T], BF16).ap()
    ident = nc.alloc_sbuf_tensor("ident", [128, 128], F32).ap()
    identb = nc.alloc_sbuf_tensor("identb", [128, 128], BF16).ap()
    st6 = nc.alloc_sbuf_tensor("st6", [128, 2, 2, 6], F32).ap()
    mv4 = nc.alloc_sbuf_tensor("mv4", [128, 2, 4, 2], F32).ap()
    sd4 = nc.alloc_sbuf_tensor("sd4", [128, 2, 4, 3], F32).ap()
    epsc = nc.alloc_sbuf_tensor("epsc", [128, 1], F32).ap()
    onec = nc.alloc_sbuf_tensor("onec", [128, 1], F32).ap()
    selt = nc.alloc_sbuf_tensor("selt", [128, 8], F32).ap()
    wselt = nc.alloc_sbuf_tensor("wselt", [128, 8], F32).ap()
    ARENA_W = 27400
    arena = nc.alloc_sbuf_tensor("arena", [128, ARENA_W], F32).ap()
    PS = nc.alloc_psum_tensor("PS", [128, 4096], F32).ap()

    class Arena:
        def __init__(self):
            self.off = 0

        def take(self, nbytes, dtype, shape_str=None, **kw):
            assert nbytes % 4 == 0
            w = nbytes // 4
            assert self.off + w <= ARENA_W, (self.off, w)
            v = arena[:, self.off:self.off + w]
            self.off += w
            if dtype == BF16:
                v = v.bitcast(BF16)
            if shape_str:
                v = v.rearrange(shape_str, **kw)
            return v

    GBh = {"ap": None}
    bX = [Buf(f"X{t}", keep=True) for t in range(NT)]
    bXT = [Buf(f"XT{t}") for t in range(NT)]
    bGB = Buf("GB", keep=True)
    bConst = Buf("const")
    bOut = Buf("out", keep=True)
    bSt = [Buf("st0"), Buf("st1")]
    bMv = [Buf("mv0"), Buf("mv1")]
    bSd = [Buf("sd0"), Buf("sd1")]

    K.op(pool, lambda: G.memset(ident, 0.0), writes=[bConst])
    K.op(pool, lambda: G.affine_select(out=ident, in_=ident, pattern=[[-1, 128]], compare_op=ALU.not_equal,
                                       fill=1.0, base=0, channel_multiplier=1), writes=[bConst])
    K.op(pool, lambda: G.tensor_copy(out=identb, in_=ident), reads=[bConst], writes=[bConst])
    K.op(pool, lambda: G.memset(epsc, EPS), writes=[bConst])
    K.op(pool, lambda: G.memset(onec, 1.0), writes=[bConst])
    bSel = Buf("sel", keep=True)
    K.dma(sp, selt, sel_d, [], [bSel], bSel)
    bWsel = Buf("wsel", keep=True)
    K.dma(sp, wselt, wsel_d, [], [bWsel], bWsel)

    xv = x_d.rearrange("(t p) d -> p t d", p=128)
    for t in range(NT):
        K.dma(sp, X[:, t, :], xv[:, t, :], [], [bX[t]], bX[t])

    def transposes_to_XT(tt, psbuf, ps_ap):
        pv = ps_ap.rearrange("p (c q) -> p c q", c=KC)
        K.mm([(lambda c=c: P.transpose(out=pv[:, c, :], in_=X[:, tt, c * 128:(c + 1) * 128], identity=ident))
              for c in range(KC)], reads=[bX[tt], bConst], writes=[psbuf])
        K.op(act, lambda: S.activation(out=XT[:, :, tt * 128:(tt + 1) * 128], in_=pv, func=AF.Copy),
             reads=[psbuf], writes=[bXT[tt]])

    ln_state = {"idx": 0}

    def load_ln_params(i, j, GB):
        GBh["ap"] = GB
        K.dma(sp, GB[:, 0, :], lng[i, j:j + 1, :].to_broadcast([128, D]), [], [bGB], bGB)
        K.dma(sp, GB[:, 1, :], lnb[i, j:j + 1, :].to_broadcast([128, D]), [], [bGB], bGB)

    def resid_accum(tt, ybuf, y_ap, first):
        if first:
            K.op(dve, lambda: V.scalar_tensor_tensor(out=X[:, tt, :], in0=X[:, tt, :], scalar=ALPHA, in1=y_ap,
                                                     op0=ALU.mult, op1=ALU.add), reads=[ybuf], writes=[bX[tt]])
        else:
            K.op(dve, lambda: V.tensor_tensor(out=X[:, tt, :], in0=X[:, tt, :], in1=y_ap, op=ALU.add),
                 reads=[ybuf], writes=[bX[tt]])

    def ln_stats(tt):
        par = (tt // 4) % 2
        j = tt % 4
        st = st6[:, j % 2]
        bs = bSt[j % 2]
        K.op(dve, lambda: V.bn_stats(out=st[:, 0, :], in_=X[:, tt, 0:512]), reads=[bX[tt]], writes=[bs])
        K.op(dve, lambda: V.bn_stats(out=st[:, 1, :], in_=X[:, tt, 512:1024]), reads=[bX[tt], bs], writes=[bs])
        K.op(dve, lambda: V.bn_aggr(out=mv4[:, par, j, :], in_=st.rearrange("p a b -> p (a b)")), reads=[bs], writes=[bMv[par]])

    def ln_finish(tb, final_out=False, tr_ps=None):
        par = tb % 2
        sd = sd4[:, par]
        K.op(act, lambda: S.activation(out=sd[:, :, 0], in_=mv4[:, par, :, 1], func=AF.Sqrt, bias=epsc, scale=1.0),
             reads=[bMv[par], bConst], writes=[bSd[par]])
        K.op(dve, lambda: V.reciprocal(out=sd[:, :, 1], in_=sd[:, :, 0]), reads=[bSd[par]], writes=[bSd[par]])
        K.op(dve, lambda: V.scalar_tensor_tensor(out=sd[:, :, 2], in0=mv4[:, par, :, 0], scalar=-1.0, in1=sd[:, :, 1],
                                                 op0=ALU.mult, op1=ALU.mult), reads=[bSd[par], bMv[par]], writes=[bSd[par]])
        for j in range(4):
            tt = 4 * tb + j
            K.op(act, lambda: S.activation(out=X[:, tt, :], in_=X[:, tt, :], func=AF.Identity,
                                           bias=sd[:, j, 2:3], scale=sd[:, j, 1:2]),
                 reads=[bSd[par]], writes=[bX[tt]])
            K.op(dve, lambda: V.tensor_tensor(out=X[:, tt, :], in0=X[:, tt, :], in1=GBh["ap"][:, 0, :], op=ALU.mult),
                 reads=[bGB], writes=[bX[tt]])
            K.op(dve, lambda: V.tensor_tensor(out=X[:, tt, :], in0=X[:, tt, :], in1=GBh["ap"][:, 1, :], op=ALU.add),
                 reads=[bGB], writes=[bX[tt]])
            if final_out:
                K.dma(sp, out_d.rearrange("(t p) d -> p t d", p=128)[:, tt, :], X[:, tt, :], [bX[tt]], [], bOut)

    bPS0 = [Buf("ps_init0"), Buf("ps_init1")]
    for tt in range(NT):
        transposes_to_XT(tt, bPS0[tt % 2], PS[:, (tt % 2) * 1024:(tt % 2) * 1024 + 1024])

    sub_count = {"n": 0}
    total_sub = 3 * len(layers)
    sync_state = {"n": 0}
    SYT = nc.alloc_sbuf_tensor("SYT", [128, 8], F32).ap()
    bSYT = Buf("SYT", keep=True)
    K.op(pool, lambda: G.memset(SYT, 0.0), writes=[bSYT])

    def sync_cores():
        i = sync_state["n"]
        sync_state["n"] += 1
        si = nc.dram_tensor(f"sync_i{i}", [128, 8], F32)
        so = nc.dram_tensor(f"sync_o{i}", [8 * 128, 8], F32)
        bi, bo = Buf(), Buf()
        K.dma(pool, si.ap(), SYT, [bSYT], [bi], bi)
        K._pre(pool, [bi], [bo])
        cc = G.collective_compute("AllGather", ALU.bypass, replica_groups=[list(range(NCORES))],
                                  ins=[si.ap().opt()], outs=[so.ap().opt()])
        K._post(pool.mark(cc), [bi], [bo])

    def done():
        return n_sub is not None and sub_count["n"] >= n_sub

    def ffn_phase(w_in, w_out, li, lnj, is_final):
        K.barrier()
        sync_cores()
        A = Arena()
        hT = A.take(GMAX * T * 2, BF16, "p (f t) -> p f t", f=GMAX)
        WI = [A.take(KC * 256 * 2, BF16, "p (k two n) -> p k two n", k=KC, two=2) for _ in range(3)]
        WO = [A.take(GMAX * D * 2, BF16, "p (f n) -> p f n", f=GMAX) for _ in range(2)]
        SG = [A.take(512 * 4, F32) for _ in range(2)]
        GBa = A.take(2 * D * 4, F32, "p (a n) -> p a n", a=2)
        bWI = [Buf(f"WI{s}") for s in range(3)]
        bWO = [Buf(f"WO{s}") for s in range(2)]
        bSG = [Buf("SG0"), Buf("SG1")]
        bh = [[Buf(f"h{f}_{tb}") for tb in range(4)] for f in range(GMAX)]
        bPA = [Buf("PA0"), Buf("PA1")]
        bPB = [Buf("PB0"), Buf("PB1")]
        PA = [PS[:, 0:1024], PS[:, 1024:2048]]
        PB = [PS[:, 2048:3072], PS[:, 3072:4096]]
        load_ln_params(li, lnj, GBa)
        wiv = w_in.rearrange("(k p) (two n) -> p k two n", p=128, two=2)
        wov = w_out.rearrange("(f p) n -> p f n", p=128)

        def load_wi(f):
            s = f % 3
            for two in range(2):
                K.dma(pool, WI[s][:, :, two, :], wiv[:, :, two, f * 128:(f + 1) * 128], [], [bWI[s]], bWI[s])

        def load_wo(gi):
            f0, cnt = GROUPS[gi]
            s = gi % 2
            K.dma(pool, WO[s][:, 0:cnt, :], wov[:, f0:f0 + cnt, :], [], [bWO[s]], bWO[s])

        load_wi(0)
        load_wi(1)
        load_wo(0)
        it = 0
        pend_tr = []
        for gi, (f0, cnt) in enumerate(GROUPS):
            first, last = gi == 0, gi == len(GROUPS) - 1
            for fl in range(cnt):
                f = f0 + fl
                if f + 2 < NF:
                    load_wi(f + 2)
                s = f % 3
                for tb in range(4):
                    pa = PA[it % 2]
                    bpa = bPA[it % 2]
                    rhs_b = [bXT[4 * tb + q] for q in range(4)]
                    fns = []
                    for two in range(2):
                        for k in range(KC):
                            fns.append(lambda two=two, k=k: P.matmul(
                                pa[:, two * 512:(two + 1) * 512], lhsT=WI[s][:, k, two, :],
                                rhs=XT[:, k, tb * 512:(tb + 1) * 512], start=(k == 0), stop=(k == KC - 1)))
                    K.mm(fns, reads=[bWI[s]] + rhs_b, writes=[bpa])
                    sg = SG[it % 2]
                    K.op(act, lambda: S.activation(out=sg, in_=pa[:, 0:512], func=AF.Silu), reads=[bpa], writes=[bSG[it % 2]])
                    K.op(dve, lambda: V.scalar_tensor_tensor(out=hT[:, fl, tb * 512:(tb + 1) * 512], in0=pa[:, 512:1024],
                                                             scalar=0.5, in1=sg, op0=ALU.mult, op1=ALU.mult),
                         reads=[bpa, bSG[it % 2]], writes=[bh[fl][tb]])
                    it += 1
            if gi + 1 < len(GROUPS):
                load_wo(gi + 1)
            ws = gi % 2
            for tt in range(NT):
                pb = PB[tt % 2]
                bpb = bPB[tt % 2]
                fns = []
                for half in range(2):
                    for fl in range(cnt):
                        fns.append(lambda half=half, fl=fl: P.matmul(
                            pb[:, half * 512:(half + 1) * 512], lhsT=hT[:, fl, tt * 128:(tt + 1) * 128],
                            rhs=WO[ws][:, fl, half * 512:(half + 1) * 512], start=(fl == 0), stop=(fl == cnt - 1)))
                K.mm(fns, reads=[bWO[ws]] + [bh[fl][tt // 4] for fl in range(cnt)], writes=[bpb])
                resid_accum(tt, bpb, pb, first)
                if last:
                    ln_stats(tt)
                    if tt % 4 == 3:
                        ln_finish(tt // 4, final_out=is_final)
                        pend_tr.append(tt // 4)
                    if tt % 4 == 1 and pend_tr:
                        tb0 = pend_tr.pop(0)
                        for j in range(4):
                            transposes_to_XT(4 * tb0 + j, bPA[j % 2], PA[j % 2])
            if last:
                while pend_tr:
                    tb0 = pend_tr.pop(0)
                    for j in range(4):
                        transposes_to_XT(4 * tb0 + j, bPA[j % 2], PA[j % 2])
        sub_count["n"] += 1

    def attn_phase(j, li):
        K.barrier()
        A = Arena()
        WQ = A.take(KC * 1536 * 2, BF16, "p (k n) -> p k n", k=KC)
        WOo = A.take(KC * D * 2, BF16, "p (k n) -> p k n", k=KC)
        GBa = A.take(2 * D * 4, F32, "p (a n) -> p a n", a=2)
        NKS = 7
        kTa = A.take(NKS * 4 * 128 * 2, BF16, "p (t g q) -> p t g q", t=NKS, g=4)
        Va = A.take(NKS * 256 * 2, BF16, "p (t n) -> p t n", t=NKS)
        COS = A.take(NT * 32 * 4, F32, "p (t n) -> p t n", t=NT)
        SIN = A.take(NT * 32 * 4, F32, "p (t n) -> p t n", t=NT)
        MK = A.take(2 * 256 * 4, F32, "p (a n) -> p a n", a=2)
        SK = A.take(16 * 4, F32)
        SKp = A.take(16 * 4, F32, "p (g s) -> p g s", g=4)
        QKr = A.take(20 * 64 * 2, BF16, "p (h d) -> p h d", h=20)
        KD = A.take(4 * 2 * 64 * 2, BF16, "p (g r d) -> p g r d", g=4, r=2)
        TA = A.take(20 * 32 * 4, F32, "p (h d) -> p h d", h=20)
        TB = A.take(20 * 32 * 4, F32, "p (h d) -> p h d", h=20)
        qT = [A.take(KC * 128 * 2, BF16, "p (c q) -> p c q", c=KC) for _ in range(3)]
        SM = [A.take(4 * 264 * 4, F32, "p (h n) -> p h n", h=4) for _ in range(2)]
        PP = [A.take(4 * 288 * 2, BF16, "p (h n) -> p h n", h=4) for _ in range(3)]
        PT = [A.take(8 * 128 * 2, BF16, "p (a q) -> p a q", a=8) for _ in range(2)]
        OT = [A.take(KC * 128 * 2, BF16, "p (c q) -> p c q", c=KC) for _ in range(2)]
        SMALL = [A.take(3 * 4 * 4, F32, "p (a n) -> p a n", a=3) for _ in range(4)]
        HALO = A.take(512 * 4, F32)
        GATH = [A.take(512 * 4, F32)] * 2
        bWQ, bWOo, bCS, bMK, bSK = Buf("WQ"), Buf("WOo"), Buf("CS"), Buf("MK"), Buf("SK")
        bkT = [Buf(f"kT{t}") for t in range(NKS)]
        bV = [Buf(f"V{t}") for t in range(NKS)]
        bQKr, bKD, bTA, bTB = Buf("QKr"), Buf("KD"), Buf("TA"), Buf("TB")
        bqT = [Buf("qT0"), Buf("qT1"), Buf("qT2")]
        bSM = [Buf("SM0"), Buf("SM1")]
        bPP = [Buf("PP0"), Buf("PP1"), Buf("PP2")]
        bPT = [Buf("PT0"), Buf("PT1")]
        bSmall = [Buf(f"SMALL{i}") for i in range(4)]
        bOT = [Buf("OT0"), Buf("OT1")]
        bHALO = Buf("HALO")
        _bg = Buf("GATH0")
        bGATH = [_bg, _bg]
        R0, R1, R2, R3 = PS[:, 0:1536], PS[:, 1536:2048], PS[:, 2048:3072], PS[:, 3072:4096]
        bR0, bR1, bR2, bR3 = Buf("R0"), Buf("R1"), Buf("R2"), Buf("R3")
        agi = nc.dram_tensor(f"agi_kv{j}", [128, 512], F32)
        ago = nc.dram_tensor(f"ago_kv{j}", [8 * 128, 512], F32)
        bAgi, bAgo = Buf("agi"), Buf("ago")

        def kslot(t):
            if t < 0:
                return 5
            if t == 15:
                return 6
            return t % 5

        def qslot(t):
            return 2 if t == 15 else t % 2

        load_ln_params(li, 1, GBa)
        K.dma(pool, WQ, wqkv[j].rearrange("(k p) n -> p k n", p=128), [], [bWQ], bWQ)
        K.dma(pool, WOo, wo_d[j].rearrange("(k p) n -> p k n", p=128), [], [bWOo], bWOo)
        K.dma(sp, COS, cos_d.rearrange("(t p) n -> p t n", p=128), [], [bCS], bCS)
        K.dma(sp, SIN, sin_d.rearrange("(t p) n -> p t n", p=128), [], [bCS], bCS)
        K.dma(sp, MK[:, 0, :], mask0_d, [], [bMK], bMK)
        K.dma(sp, MK[:, 1, :], maskn_d, [], [bMK], bMK)
        K.dma(sp, SK, sinks_d[j:j + 1, :].to_broadcast([128, 16]), [], [bSK], bSK)
        K.op(dve, lambda: V.tensor_copy(out=SKp.rearrange("p g (b a) -> p g b a", b=2),
                                        in_=SK.rearrange("p (g a b) -> p g b a", g=4, a=2, b=2)),
             reads=[bSK], writes=[bSK])

        r0b = R0.bitcast(BF16)

        def k_transposes(slot):
            rk = r0b[:, 1024:1536].rearrange("p (g q) -> p g q", g=4)
            K.mm([(lambda g=g: P.transpose(out=rk[:, g, :], in_=KD[:, g].rearrange("p r d -> p (r d)"), identity=identb))
                  for g in range(4)], reads=[bKD, bConst], writes=[bR0])
            K.op(act, lambda: S.activation(out=kTa[:, slot], in_=rk, func=AF.Copy), reads=[bR0], writes=[bkT[slot]])

        def prep(tt):
            ks, qs = kslot(tt), qslot(tt)
            fns = []
            for c in range(3):
                for k in range(KC):
                    fns.append(lambda c=c, k=k: P.matmul(R0[:, c * 512:(c + 1) * 512], lhsT=XT[:, k, tt * 128:(tt + 1) * 128],
                                                         rhs=WQ[:, k, c * 512:(c + 1) * 512], start=(k == 0), stop=(k == KC - 1)))
            K.mm(fns, reads=[bWQ, bXT[tt]], writes=[bR0])
            qk = R0[:, 0:1280].rearrange("p (h d) -> p h d", h=20)
            t1, t2 = qk[:, :, 0:32], qk[:, :, 32:64]
            cb = COS[:, tt, :].unsqueeze(1).to_broadcast([128, 20, 32])
            sb = SIN[:, tt, :].unsqueeze(1).to_broadcast([128, 20, 32])
            K.op(dve, lambda: V.tensor_tensor(out=TA, in0=t1, in1=cb, op=ALU.mult), reads=[bR0, bCS], writes=[bTA])
            K.op(dve, lambda: V.tensor_tensor(out=TB, in0=t2, in1=sb, op=ALU.mult), reads=[bR0, bCS], writes=[bTB])
            K.op(dve, lambda: V.tensor_tensor(out=QKr[:, :, 0:32], in0=TA, in1=TB, op=ALU.subtract), reads=[bTA, bTB], writes=[bQKr])
            K.op(dve, lambda: V.tensor_tensor(out=TA, in0=t2, in1=cb, op=ALU.mult), reads=[bR0, bCS], writes=[bTA])
            K.op(dve, lambda: V.tensor_tensor(out=TB, in0=t1, in1=sb, op=ALU.mult), reads=[bR0, bCS], writes=[bTB])
            K.op(dve, lambda: V.tensor_tensor(out=QKr[:, :, 32:64], in0=TA, in1=TB, op=ALU.add), reads=[bTA, bTB], writes=[bQKr])
            K.op(act, lambda: S.activation(out=Va[:, ks, :], in_=R0[:, 1280:1536], func=AF.Copy), reads=[bR0], writes=[bV[ks]])
            for r in range(2):
                K.op(dve, lambda r=r: V.tensor_copy(out=KD[:, :, r, :], in_=QKr[:, 16:20, :]), reads=[bQKr], writes=[bKD])
            rq = r0b[:, 0:1024].rearrange("p (c q) -> p c q", c=KC)
            K.mm([(lambda c=c: P.transpose(out=rq[:, c, :], in_=QKr[:, 2 * c:2 * c + 2, :].rearrange("p h d -> p (h d)"), identity=identb))
                  for c in range(KC)], reads=[bQKr, bConst], writes=[bR0])
            K.op(act, lambda: S.activation(out=qT[qs], in_=rq, func=AF.Copy), reads=[bR0], writes=[bqT[qs]])
            k_transposes(ks)

        s3 = R3.rearrange("p (h n) -> p h n", h=4)
        r1b = R1.bitcast(BF16).rearrange("p (a q) -> p a q", a=8)
        r2f = R2.rearrange("p (c q) -> p c q", c=KC)

        def heads(g, sl):
            b_, a2 = sl // 2, sl % 2
            h = 4 * g + 2 * a2 + b_
            return h, b_ * 64

        def st0(u):
            qb, g = divmod(u, 4)
            kss = [kslot(qb - 1), kslot(qb)]
            qs = qslot(qb)
            fns = []
            for sl in range(4):
                h, pb0 = heads(g, sl)
                for kb in range(2):
                    fns.append(lambda sl=sl, h=h, pb0=pb0, kb=kb: P.matmul(
                        s3[:, sl, kb * 128:(kb + 1) * 128], lhsT=qT[qs][pb0:pb0 + 64, h // 2, :],
                        rhs=kTa[pb0:pb0 + 64, kss[kb], g, :], start=True, stop=True))
            K.mm(fns, reads=[bqT[qs], bkT[kss[0]], bkT[kss[1]]], writes=[bR3])
            if g == 0 and qb + 1 < 15:
                prep(qb + 1)

        def st1(u):
            qb, g = divmod(u, 4)
            smb, bsm = SM[u % 2], bSM[u % 2]
            sml, bsl = SMALL[u % 4], bSmall[u % 4]
            mk = MK[:, 0 if qb == 0 else 1, :]
            K.op(dve, lambda: V.tensor_copy(out=smb[:, :, 256], in_=SKp[:, g, :]), reads=[bSK], writes=[bsm])
            K.op(dve, lambda: V.scalar_tensor_tensor(out=smb[:, :, 0:256], in0=s3, scalar=0.125,
                                                     in1=mk.unsqueeze(1).to_broadcast([128, 4, 256]), op0=ALU.mult, op1=ALU.add),
                 reads=[bR3, bMK], writes=[bsm])
            K.op(dve, lambda: V.tensor_reduce(out=sml[:, 0, :], in_=smb[:, :, 0:257], axis=AX.X, op=ALU.max, negate=True),
                 reads=[bsm], writes=[bsl])

        def st2(u):
            smb, bsm = SM[u % 2], bSM[u % 2]
            sml, bsl = SMALL[u % 4], bSmall[u % 4]
            ppb, bpp = PP[u % 3], bPP[u % 3]
            for sl in range(4):
                K.op(act, lambda sl=sl: S.activation(out=ppb[:, sl, 0:257], in_=smb[:, sl, 0:257], func=AF.Exp, bias=sml[:, 0, sl:sl + 1],
                                                     scale=1.0, accum_out=sml[:, 1, sl:sl + 1]),
                     reads=[bsm, bsl], writes=[bpp, bsl])

        def st3(u):
            sml, bsl = SMALL[u % 4], bSmall[u % 4]
            ppb, bpp = PP[u % 3], bPP[u % 3]
            K.op(dve, lambda: V.reciprocal(out=sml[:, 2, :], in_=sml[:, 1, :]), reads=[bsl], writes=[bsl])
            K.op(dve, lambda: V.tensor_tensor(out=ppb[:, :, 0:256], in0=ppb[:, :, 0:256],
                                              in1=sml[:, 2, :].unsqueeze(2).to_broadcast([128, 4, 256]), op=ALU.mult),
                 reads=[bsl], writes=[bpp])

        def st4(u):
            ppb, bpp = PP[u % 3], bPP[u % 3]
            K.mm([(lambda sl=sl, kb=kb: P.transpose(out=r1b[:, 2 * sl + kb, :], in_=ppb[:, sl, kb * 128:(kb + 1) * 128], identity=identb))
                  for sl in range(4) for kb in range(2)], reads=[bpp, bConst], writes=[bR1])

        def st5(u):
            K.op(act, lambda: S.activation(out=PT[u % 2], in_=r1b, func=AF.Copy), reads=[bR1], writes=[bPT[u % 2]])

        def st6(u):
            qb, g = divmod(u, 4)
            kss = [kslot(qb - 1), kslot(qb)]
            ptb, bpt = PT[u % 2], bPT[u % 2]
            fns = []
            for sl in range(4):
                h, pb0 = heads(g, sl)
                for kb in range(2):
                    fns.append(lambda sl=sl, h=h, pb0=pb0, kb=kb: P.matmul(
                        r2f[pb0:pb0 + 64, h // 2, :], lhsT=Va[:, kss[kb], g * 64:(g + 1) * 64],
                        rhs=ptb[:, 2 * sl + kb, :], start=(kb == 0), stop=(kb == 1)))
            K.mm(fns, reads=[bpt, bV[kss[0]], bV[kss[1]]], writes=[bR2])
            if g == 3:
                block_end(qb)

        def block_end(qb):
            os_ = qb % 2
            K.op(act, lambda: S.activation(out=OT[os_], in_=r2f, func=AF.Copy), reads=[bR2], writes=[bOT[os_]])
            y = R0[:, 0:1024]
            fns = []
            for half in range(2):
                for c in range(KC):
                    fns.append(lambda half=half, c=c: P.matmul(y[:, half * 512:(half + 1) * 512], lhsT=OT[os_][:, c, :],
                                                               rhs=WOo[:, c, half * 512:(half + 1) * 512], start=(c == 0), stop=(c == KC - 1)))
            K.mm(fns, reads=[bOT[os_], bWOo], writes=[bR0])
            resid_accum(qb, bR0, y, True)
            ln_stats(qb)
            if qb % 4 == 3:
                ln_finish(qb // 4)
                for jj in range(4):
                    transposes_to_XT(4 * (qb // 4) + jj, bR0, R0[:, 0:1024])

        prep(15)
        K.op(dve, lambda: V.tensor_copy(out=HALO[:, 0:256], in_=QKr[:, 16:20, :].rearrange("p h d -> p (h d)")),
             reads=[bQKr], writes=[bHALO])
        K.op(dve, lambda: V.tensor_copy(out=HALO[:, 256:512], in_=Va[:, 6, :]), reads=[bV[6]], writes=[bHALO])
        K.dma(pool, agi.ap(), HALO, [bHALO], [bAgi], bAgi)
        K._pre(pool, [bAgi], [bAgo])
        cc = G.collective_compute("AllGather", ALU.bypass, replica_groups=[list(range(NCORES))],
                                  ins=[agi.ap().opt()], outs=[ago.ap().opt()])
        K._post(pool.mark(cc), [bAgi], [bAgo])
        agv = ago.ap().rearrange("(r p) n -> r p n", p=128)
        for r in range(8):
            gb_, bg_ = GATH[r % 2], bGATH[r % 2]
            K.dma(pool, gb_, agv[r], [bAgo], [bg_], bg_)
            if r == 0:
                K.op(dve, lambda: V.tensor_scalar(out=HALO, in0=gb_, scalar1=selt[:, 0:1], scalar2=None, op0=ALU.mult),
                     reads=[bg_, bSel], writes=[bHALO])
            else:
                K.op(dve, lambda: V.scalar_tensor_tensor(out=HALO, in0=gb_, scalar=selt[:, r:r + 1], in1=HALO,
                                                         op0=ALU.mult, op1=ALU.add), reads=[bg_, bSel], writes=[bHALO])
        for r in range(2):
            K.op(dve, lambda r=r: V.tensor_copy(out=KD[:, :, r, :], in_=HALO[:, 0:256].rearrange("p (g d) -> p g d", g=4)),
                 reads=[bHALO], writes=[bKD])
        K.op(dve, lambda: V.tensor_copy(out=Va[:, 5, :], in_=HALO[:, 256:512]), reads=[bHALO], writes=[bV[5]])
        k_transposes(5)
        prep(0)
        NG = 64
        stages = [st0, st1, st2, st3, st4, st5, st6]
        import os
        if True:
            for u in range(NG):
                for s_ in range(7):
                    stages[s_](u)
        else:
            for t in range(NG + 6):
                for s_ in (6, 5, 4, 3, 2, 1, 0):
                    u = t - s_
                    if 0 <= u < NG:
                        stages[s_](u)
        sub_count["n"] += 1

    def lru_full(j, li):
        K.barrier()
        A = Arena()
        G1 = A.take(KC * T * 2, BF16, "p (c t) -> p c t", c=KC)
        G2 = A.take(KC * T * 2, BF16, "p (c t) -> p c t", c=KC)
        wreg0 = A.off
        WX = A.take(KC * 256 * 2, BF16, "p (k n) -> p k n", k=KC)
        WGt = A.take(KC * 256 * 2, BF16, "p (k n) -> p k n", k=KC)
        treg0 = A.off
        T2 = A.take(2 * 512 * 4, F32, "p (c t) -> p c t", c=2)
        T4 = A.take(2 * 512 * 4, F32, "p (c t) -> p c t", c=2)
        T5 = A.take(2 * 512 * 4, F32, "p (c t) -> p c t", c=2)
        T1 = A.take(2 * 512 * 4, F32, "p (c t) -> p c t", c=2)
        WRA = [A.take(2 * 256 * 2, BF16, "p (c n) -> p c n", c=2) for _ in range(2)]
        WRX = [A.take(2 * 256 * 2, BF16, "p (c n) -> p c n", c=2) for _ in range(2)]
        XB = A.take(2 * 516 * 4, F32, "p (c t) -> p c t", c=2)
        T3 = A.take(2 * 512 * 2, BF16, "p (c t) -> p c t", c=2)
        ZER = A.take(512 * 2, BF16)
        PR = A.take(12 * 8 * 4, F32, "p (a c) -> p a c", a=12)
        XTH = A.take(KC * 4 * 2, BF16, "p (k t) -> p k t", k=KC)
        HX = A.take(32 * 4, F32)
        GX = A.take(8 * 32 * 4, F32, "p (r n) -> p r n", r=8)
        CAR = A.take(16 * 4, F32)
        GC = A.take(8 * 16 * 4, F32, "p (r n) -> p r n", r=8)
        CIN = A.take(8 * 4, F32)
        CT = A.take(8 * 4, F32)
        HL = A.take(2 * 4 * 4, F32, "p (c t) -> p c t", c=2)
        bWX, bWG, bWRA, bWRX = Buf("WX"), Buf("WG"), [Buf(), Buf()], [Buf(), Buf()]
        bXB, bT1, bT2, bT3, bT4, bT5, bZ, bPR = Buf("XB"), Buf("T1"), Buf("T2"), Buf("T3"), Buf("T4"), Buf("T5"), Buf("Z"), Buf("PR")
        bXTH, bHX, bGX, bCAR, bGC, bCIN, bHL = Buf(), Buf(), Buf(), Buf(), Buf(), Buf(), Buf()
        bG = [[Buf(f"G{c}_{tb}") for tb in range(4)] for c in range(KC)]
        L0, L1, L2, L3 = PS[:, 0:1024], PS[:, 1024:2048], PS[:, 2048:3072], PS[:, 3072:4096]
        bL0, bL1, bL2, bL3 = Buf("L0"), Buf("L1"), Buf("L2"), Buf("L3")
        agx_i = nc.dram_tensor(f"agx_i{j}", [128, 32], F32)
        agx_o = nc.dram_tensor(f"agx_o{j}", [8 * 128, 32], F32)
        agc_i = nc.dram_tensor(f"agc_i{j}", [128, 16], F32)
        agc_o = nc.dram_tensor(f"agc_o{j}", [8 * 128, 16], F32)
        bAxi, bAxo, bAci, bAco = Buf(), Buf(), Buf(), Buf()

        K.op(dve, lambda: V.memset(HX, 0.0), writes=[bHX])
        K.op(dve, lambda: V.tensor_copy(out=HX[:, 0:24].rearrange("p (k t) -> p k t", k=KC), in_=XT[:, :, T - 3:T]),
             reads=[bXT[15]], writes=[bHX])
        K.dma(pool, agx_i.ap(), HX, [bHX], [bAxi], bAxi)
        K._pre(pool, [bAxi], [bAxo])
        cc = G.collective_compute("AllGather", ALU.bypass, replica_groups=[list(range(NCORES))],
                                  ins=[agx_i.ap().opt()], outs=[agx_o.ap().opt()])
        K._post(pool.mark(cc), [bAxi], [bAxo])
        K.dma(pool, GX, agx_o.ap().rearrange("(r p) n -> p r n", p=128), [bAxo], [bGX], bGX)
        with nc.allow_non_contiguous_dma(reason="tiny per-channel parameter vectors"):
            for w in range(4):
                K.dma(sp, PR[:, w, :], lcw[j, w].rearrange("(c p) -> p c", p=128), [], [bPR], bPR)
            K.dma(sp, PR[:, 4, :], lcb[j].rearrange("(c p) -> p c", p=128), [], [bPR], bPR)
            K.dma(sp, PR[:, 5, :], lbra[j].rearrange("(c p) -> p c", p=128), [], [bPR], bPR)
            K.dma(sp, PR[:, 6, :], lbrx[j].rearrange("(c p) -> p c", p=128), [], [bPR], bPR)
            K.dma(sp, PR[:, 7, :], llam[j].rearrange("(c p) -> p c", p=128), [], [bPR], bPR)
        K.op(act, lambda: S.activation(out=PR[:, 10, :], in_=PR[:, 7, :], func=AF.Exp, scale=-1.0), reads=[bPR], writes=[bPR])
        K.op(act, lambda: S.activation(out=PR[:, 11, :], in_=PR[:, 10, :], func=AF.Ln, bias=onec, scale=1.0), reads=[bPR, bConst], writes=[bPR])
        K.op(dve, lambda: V.tensor_scalar(out=PR[:, 8, :], in0=PR[:, 11, :], scalar1=-8.0, scalar2=None, op0=ALU.mult), reads=[bPR], writes=[bPR])
        K.op(dve, lambda: V.tensor_scalar(out=PR[:, 9, :], in0=PR[:, 11, :], scalar1=-16.0, scalar2=None, op0=ALU.mult), reads=[bPR], writes=[bPR])
        K.op(dve, lambda: V.memset(ZER, 0.0), writes=[bZ])
        K.op(dve, lambda: V.tensor_scalar(out=HX, in0=GX[:, 0, :], scalar1=selt[:, 0:1], scalar2=None, op0=ALU.mult),
             reads=[bGX, bSel], writes=[bHX])
        for r in range(1, 8):
            K.op(dve, lambda r=r: V.scalar_tensor_tensor(out=HX, in0=GX[:, r, :], scalar=selt[:, r:r + 1], in1=HX,
                                                         op0=ALU.mult, op1=ALU.add), reads=[bGX, bSel], writes=[bHX])
        K.op(dve, lambda: V.memset(XTH, 0.0), writes=[bXTH])
        K.op(dve, lambda: V.tensor_copy(out=XTH[:, :, 0:3], in_=HX[:, 0:24].rearrange("p (k t) -> p k t", k=KC)),
             reads=[bHX], writes=[bXTH])

        wv = lwin[j].rearrange("(k p) n -> p k n", p=128)

        def load_w(n):
            s = n % 2
            K.dma(pool, WX, wv[:, :, n * 256:(n + 1) * 256], [], [bWX], bWX)
            K.dma(pool, WGt, wv[:, :, 1024 + n * 256:1024 + (n + 1) * 256], [], [bWG], bWG)
            K.dma(pool, WRA[s], lwra[j, n].rearrange("(c p) o -> p c o", p=128), [], [bWRA[s]], bWRA[s])
            K.dma(pool, WRX[s], lwrx[j, n].rearrange("(c p) o -> p c o", p=128), [], [bWRX[s]], bWRX[s])

        for n in range(4):
            load_w(n)
            s = n % 2
            for tb in range(4):
                rhs_b = [bXT[4 * tb + q] for q in range(4)]
                l0 = L0.rearrange("p (c t) -> p c t", c=2)
                fns = []
                for c in range(2):
                    for k in range(KC):
                        fns.append(lambda c=c, k=k: P.matmul(l0[:, c, :], lhsT=WX[:, k, c * 128:(c + 1) * 128],
                                                             rhs=XT[:, k, tb * 512:(tb + 1) * 512], start=(k == 0), stop=(k == KC - 1)))
                K.mm(fns, reads=[bWX] + rhs_b, writes=[bL0])
                if tb == 0:
                    l3h = L3[:, 0:8].rearrange("p (c t) -> p c t", c=2)
                    fns = []
                    for c in range(2):
                        for k in range(KC):
                            fns.append(lambda c=c, k=k: P.matmul(l3h[:, c, :], lhsT=WX[:, k, c * 128:(c + 1) * 128],
                                                                 rhs=XTH[:, k, :], start=(k == 0), stop=(k == KC - 1)))
                    K.mm(fns, reads=[bWX, bXTH], writes=[bL3])
                    K.op(dve, lambda: V.tensor_copy(out=XB[:, :, 0:3], in_=l3h[:, :, 0:3]), reads=[bL3], writes=[bXB])
                else:
                    K.op(dve, lambda: V.tensor_copy(out=XB[:, :, 0:3], in_=XB[:, :, 512:515]), reads=[], writes=[bXB])
                K.op(act, lambda: S.activation(out=XB[:, :, 3:515], in_=l0, func=AF.Copy), reads=[bL0], writes=[bXB])
                for c in range(2):
                    ch = 2 * n + c
                    K.op(dve, lambda c=c, ch=ch: V.tensor_scalar(out=T2[:, c, :], in0=XB[:, c, 3:515], scalar1=PR[:, 3, ch:ch + 1],
                                                                 scalar2=PR[:, 4, ch:ch + 1], op0=ALU.mult, op1=ALU.add),
                         reads=[bXB, bPR], writes=[bT2])
                    for w in range(3):
                        K.op(dve, lambda c=c, ch=ch, w=w: V.scalar_tensor_tensor(
                            out=T2[:, c, :], in0=XB[:, c, w:w + 512], scalar=PR[:, w, ch:ch + 1], in1=T2[:, c, :],
                            op0=ALU.mult, op1=ALU.add), reads=[bXB, bPR], writes=[bT2])
                K.op(dve, lambda: V.tensor_copy(out=T3, in_=T2), reads=[bT2], writes=[bT3])
                l1 = L1.rearrange("p (c t) -> p c t", c=2)
                l2 = L2.rearrange("p (c t) -> p c t", c=2)
                for (lw, bw, lps, bps) in ((WRA[s], bWRA[s], l1, bL1), (WRX[s], bWRX[s], l2, bL2)):
                    fns = []
                    for jo in range(2):
                        for ci in range(2):
                            fns.append(lambda jo=jo, ci=ci, lw=lw, lps=lps: P.matmul(
                                lps[:, jo, :], lhsT=lw[:, ci, jo * 128:(jo + 1) * 128], rhs=T3[:, ci, :],
                                start=(ci == 0), stop=(ci == 1)))
                    K.mm(fns, reads=[bw, bT3], writes=[bps])
                for c in range(2):
                    ch = 2 * n + c
                    K.op(act, lambda c=c, ch=ch: S.activation(out=T4[:, c, :], in_=l1[:, c, :], func=AF.Sigmoid,
                                                              bias=PR[:, 5, ch:ch + 1], scale=1.0), reads=[bL1, bPR], writes=[bT4])
                for c in range(2):
                    ch = 2 * n + c
                    K.op(act, lambda c=c, ch=ch: S.activation(out=T1[:, c, :], in_=l2[:, c, :], func=AF.Sigmoid,
                                                              bias=PR[:, 6, ch:ch + 1], scale=1.0), reads=[bL2, bPR], writes=[bT1])
                for c in range(2):
                    ch = 2 * n + c
                    K.op(act, lambda c=c, ch=ch: S.activation(out=T5[:, c, :], in_=T4[:, c, :], func=AF.Exp,
                                                              scale=PR[:, 8, ch:ch + 1]), reads=[bT4, bPR], writes=[bT5])
                for c in range(2):
                    ch = 2 * n + c
                    K.op(act, lambda c=c, ch=ch: S.activation(out=T4[:, c, :], in_=T4[:, c, :], func=AF.Exp,
                                                              scale=PR[:, 9, ch:ch + 1]), reads=[bPR], writes=[bT4])
                K.op(act, lambda: S.activation(out=T4, in_=T4, func=AF.Sqrt, bias=onec, scale=-1.0), reads=[bConst], writes=[bT4])
                K.op(dve, lambda: V.tensor_tensor(out=T1, in0=T1, in1=T2, op=ALU.mult), reads=[bT2], writes=[bT1])
                K.op(dve, lambda: V.tensor_tensor(out=T1, in0=T1, in1=T4, op=ALU.mult), reads=[bT4], writes=[bT1])
                for c in range(2):
                    init_h = 0.0 if tb == 0 else HL[:, c, 0:1]
                    init_a = 1.0 if tb == 0 else HL[:, c, 1:2]
                    K.op(dve, lambda c=c, init_h=init_h: V.tensor_tensor_scan(out=T2[:, c, :], data0=T5[:, c, :], data1=T1[:, c, :],
                                                                              initial=init_h, op0=ALU.mult, op1=ALU.add),
                         reads=[bT5, bT1, bHL], writes=[bT2])
                    K.op(dve, lambda c=c, init_a=init_a: V.tensor_tensor_scan(out=T4[:, c, :], data0=T5[:, c, :], data1=ZER,
                                                                              initial=init_a, op0=ALU.mult, op1=ALU.add),
                         reads=[bT5, bZ, bHL], writes=[bT4])
                for c in range(2):
                    K.op(dve, lambda c=c: V.tensor_copy(out=HL[:, c, 0:1], in_=T2[:, c, 511:512]), reads=[bT2], writes=[bHL])
                    K.op(dve, lambda c=c: V.tensor_copy(out=HL[:, c, 1:2], in_=T4[:, c, 511:512]), reads=[bT4], writes=[bHL])
                if tb == 3:
                    for c in range(2):
                        ch = 2 * n + c
                        K.op(dve, lambda c=c, ch=ch: V.tensor_copy(out=CAR[:, ch:ch + 1], in_=T4[:, c, 511:512]), reads=[bT4], writes=[bCAR])
                        K.op(dve, lambda c=c, ch=ch: V.tensor_copy(out=CAR[:, 8 + ch:9 + ch], in_=T2[:, c, 511:512]), reads=[bT2], writes=[bCAR])
                l3 = L3.rearrange("p (c t) -> p c t", c=2)
                fns = []
                for c in range(2):
                    for k in range(KC):
                        fns.append(lambda c=c, k=k: P.matmul(l3[:, c, :], lhsT=WGt[:, k, c * 128:(c + 1) * 128],
                                                             rhs=XT[:, k, tb * 512:(tb + 1) * 512], start=(k == 0), stop=(k == KC - 1)))
                K.mm(fns, reads=[bWG] + rhs_b, writes=[bL3])
                K.op(act, lambda: S.activation(out=T1, in_=l3, func=AF.Gelu_apprx_tanh), reads=[bL3], writes=[bT1])
                for c in range(2):
                    ch = 2 * n + c
                    K.op(dve, lambda c=c, ch=ch: V.tensor_tensor(out=G1[:, ch, tb * 512:(tb + 1) * 512], in0=T2[:, c, :], in1=T1[:, c, :], op=ALU.mult),
                         reads=[bT2, bT1], writes=[bG[ch][tb]])
                    K.op(dve, lambda c=c, ch=ch: V.tensor_tensor(out=G2[:, ch, tb * 512:(tb + 1) * 512], in0=T4[:, c, :], in1=T1[:, c, :], op=ALU.mult),
                         reads=[bT4, bT1], writes=[bG[ch][tb]])
        K.dma(pool, agc_i.ap(), CAR, [bCAR], [bAci], bAci)
        K._pre(pool, [bAci], [bAco])
        cc = G.collective_compute("AllGather", ALU.bypass, replica_groups=[list(range(NCORES))],
                                  ins=[agc_i.ap().opt()], outs=[agc_o.ap().opt()])
        K._post(pool.mark(cc), [bAci], [bAco])
        K.dma(pool, GC, agc_o.ap().rearrange("(r p) n -> p r n", p=128), [bAco], [bGC], bGC)
        K.barrier()
        WOl = arena[:, treg0:treg0 + 4096].bitcast(BF16).rearrange("p (k n) -> p k n", k=KC)
        GBa = arena[:, wreg0:wreg0 + 2048].rearrange("p (a n) -> p a n", a=2)
        bWOl = Buf("WOl")
        K.dma(pool, WOl, lwout[j].rearrange("(k p) n -> p k n", p=128), [], [bWOl], bWOl)
        load_ln_params(li, 1, GBa)
        K.op(dve, lambda: V.memset(CIN, 0.0), writes=[bCIN])
        for r in range(8):
            K.op(dve, lambda r=r: V.tensor_scalar(out=CT, in0=GC[:, r, 0:8], scalar1=-1.0, scalar2=wselt[:, r:r + 1],
                                                  op0=ALU.add, op1=ALU.mult), reads=[bGC, bWsel], writes=[bCIN])
            K.op(dve, lambda: V.scalar_tensor_tensor(out=CT, in0=CT, scalar=1.0, in1=CIN, op0=ALU.add, op1=ALU.mult),
                 reads=[], writes=[bCIN])
            K.op(dve, lambda r=r: V.scalar_tensor_tensor(out=CIN, in0=GC[:, r, 8:16], scalar=wselt[:, r:r + 1], in1=CT,
                                                         op0=ALU.mult, op1=ALU.add), reads=[bGC, bWsel], writes=[bCIN])
        for tb in range(4):
            for ch in range(KC):
                K.op(dve, lambda ch=ch, tb=tb: V.scalar_tensor_tensor(
                    out=G1[:, ch, tb * 512:(tb + 1) * 512], in0=G2[:, ch, tb * 512:(tb + 1) * 512], scalar=CIN[:, ch:ch + 1],
                    in1=G1[:, ch, tb * 512:(tb + 1) * 512], op0=ALU.mult, op1=ALU.add), reads=[bCIN], writes=[bG[ch][tb]])
        bPB = [Buf("PB0"), Buf("PB1")]
        PB = [PS[:, 2048:3072], PS[:, 3072:4096]]
        bPA = [Buf("PA0"), Buf("PA1")]
        PA = [PS[:, 0:1024], PS[:, 1024:2048]]
        pend = []
        for tt in range(NT):
            pb, bpb = PB[tt % 2], bPB[tt % 2]
            fns = []
            for half in range(2):
                for c in range(KC):
                    fns.append(lambda half=half, c=c: P.matmul(pb[:, half * 512:(half + 1) * 512], lhsT=G1[:, c, tt * 128:(tt + 1) * 128],
                                                               rhs=WOl[:, c, half * 512:(half + 1) * 512], start=(c == 0), stop=(c == KC - 1)))
            K.mm(fns, reads=[bWOl] + [bG[c][tt // 4] for c in range(KC)], writes=[bpb])
            resid_accum(tt, bpb, pb, True)
            ln_stats(tt)
            if tt % 4 == 3:
                ln_finish(tt // 4)
                pend.append(tt // 4)
            if tt % 4 == 1 and pend:
                tb0 = pend.pop(0)
                for jj in range(4):
                    transposes_to_XT(4 * tb0 + jj, bPA[jj % 2], PA[jj % 2])
        while pend:
            tb0 = pend.pop(0)
            for jj in range(4):
                transposes_to_XT(4 * tb0 + jj, bPA[jj % 2], PA[jj % 2])
        sub_count["n"] += 1

    nl = len(layers)
    for idx, i in enumerate(layers):
        last_layer = idx == nl - 1
        if done():
            break
        ffn_phase(w1i[i], w1o[i], i, 0, is_final=(n_sub is not None and sub_count["n"] + 1 == n_sub))
        if done():
            break
        if i % 2 == 0:
            attn_phase(i // 2, i)
        else:
            lru_full(i // 2, i)
        if done():
            break
        fin = last_layer or (n_sub is not None and sub_count["n"] + 1 == n_sub)
        ffn_phase(w2i[i], w2o[i], i, 2, is_final=fin)
    if n_sub is not None and bOut.dsem is None:
        K.barrier()
        for tt in range(NT):
            K.dma(sp, out_d.rearrange("(t p) d -> p t d", p=128)[:, tt, :], X[:, tt, :], [bX[tt]], [], bOut)
    sp.wait(Tok(bOut.dsem, bOut.dcnt, "dma"))
    return nc


def host_tables(core):
    b, q = core // 4, core % 4
    pos = (np.arange(T, dtype=np.float32) + np.float32(q * T)).astype(np.float32)
    inv_freq = (np.float32(10000.0) ** (-np.arange(0, 64, 2, dtype=np.float32) / np.float32(64))).astype(np.float32)
    ang = (pos[:, None] * inv_freq[None, :]).astype(np.float32)
    cos_t, sin_t = np.cos(ang).astype(np.float32), np.sin(ang).astype(np.float32)
    i = np.arange(128)[:, None]
    jj = np.arange(128)[None, :]
    prev = np.where(jj > i, 0.0, NEG).astype(np.float32)
    cur = np.where(jj <= i, 0.0, NEG).astype(np.float32)
    maskn = np.concatenate([prev, cur], axis=1)
    mask0 = maskn.copy()
    if q == 0:
        mask0[:, 0:128] = NEG
    sel = np.zeros((128, 8), np.float32)
    if q > 0:
        sel[:, core - 1] = 1.0
    wsel = np.zeros((128, 8), np.float32)
    for r in range(b * 4, core):
        wsel[:, r] = 1.0
    return dict(cos_t=cos_t, sin_t=sin_t, mask0=mask0, maskn=maskn, sel=sel, wsel=wsel)


_W_NAMES = ["ffn1_w_in", "ffn1_w_out", "ffn2_w_in", "ffn2_w_out", "ln_g", "ln_b", "attn_w_qkv", "attn_sinks", "attn_w_o",
            "lru_w_in", "lru_conv_w", "lru_conv_b", "lru_w_ra", "lru_b_ra", "lru_w_rx", "lru_b_rx", "lru_lambda", "lru_w_out"]


def run(inputs, layers=(0, 1, 2, 3), n_sub=None):
    x = np.ascontiguousarray(np.asarray(inputs["x"], dtype=np.float32)).reshape(NCORES, T, D)
    shared = {k: np.ascontiguousarray(np.asarray(inputs[k], dtype=np.float32)) for k in _W_NAMES}
    nc = build_program(layers, n_sub)
    in_maps = []
    for c in range(NCORES):
        m = dict(shared)
        m["x"] = x[c]
        m.update(host_tables(c))
        in_maps.append(m)
    res = run_bass_kernel_spmd(nc, in_maps, core_ids=list(range(NCORES)))
    out = np.stack([np.asarray(res.results[c]["out"]) for c in range(NCORES)], axis=0)
    return out.reshape(2, 4 * T, D).astype(np.float32)


def kernel(**inputs):
    return run(inputs)
```
